# Optimizing a Trainium2 kernel written in Bass

```python
import math
import jax, jax.numpy as jnp
from jax import lax
import numpy as np

D_MODEL = 1024
BATCH = 8
SEQ = 2048
DEPTH = 4

GRID_W = 64
CTX_LEN = 256
Q_BLOCK = 128

LRU_WIDTH = 256
LRU_BLOCKS = 4
LRU_C = 8.0
LRU_CONV = 4
LRU_CONV_LEFT = 2
DIFF_HEADS = 4
DIFF_D = 32
DIFF_V = 2 * DIFF_D
GQA_Q_HEADS = 4
GQA_KV_HEADS = 2
GQA_D = 64
NA_HEADS = 4
NA_D = 64
NA_WIN_ROWS = 8
NA_WIN_COLS = 16

N_BRANCH = 4
BRANCH_W = 256
D_FF = 2816
FFN_CONV = 3
ROPE_BASE = 10000.0
ALPHA = (2 * DEPTH) ** 0.25
BETA = (8 * DEPTH) ** -0.25
EPS = 1e-6

IN_WIDTHS = (LRU_WIDTH, LRU_WIDTH,
             DIFF_HEADS * 2 * DIFF_D, DIFF_HEADS * 2 * DIFF_D, DIFF_HEADS * DIFF_V,
             GQA_Q_HEADS * GQA_D, GQA_KV_HEADS * GQA_D, GQA_KV_HEADS * GQA_D,
             NA_HEADS * NA_D, NA_HEADS * NA_D, NA_HEADS * NA_D,
             N_BRANCH * D_MODEL)
D_IN = sum(IN_WIDTHS)

kernel_name = 'hybrid_diffusion_parallel_mixer_trunk'

F32 = jnp.float32


def layer_norm(x, g=None, b=None):
    xf = x.astype(F32)
    mu = jnp.mean(xf, -1, keepdims=True)
    var = jnp.mean(jnp.square(xf - mu), -1, keepdims=True)
    y = (xf - mu) * lax.rsqrt(var + EPS)
    if g is not None:
        y = y * g + b
    return y.astype(x.dtype)


def rms_norm(x, g):
    xf = x.astype(F32)
    return (xf * lax.rsqrt(jnp.mean(xf * xf, -1, keepdims=True) + EPS) * g).astype(x.dtype)


def modulate(x, shift, scale):
    return layer_norm(x) * (1 + scale) + shift


def dwconv(x, w, b, left):
    K, C = w.shape
    y = lax.conv_general_dilated(x, w[:, None, :], window_strides=(1,),
                                 padding=[(left, K - 1 - left)],
                                 dimension_numbers=('NWC', 'WIO', 'NWC'),
                                 feature_group_count=C)
    return y + b


def axial_rope(n_tok, dim):
    t = jnp.arange(n_tok)
    row = (t // GRID_W).astype(F32)
    col = (t % GRID_W).astype(F32)
    n_freq = dim // 4
    inv = ROPE_BASE ** (-jnp.arange(n_freq, dtype=F32) / n_freq)
    ang = jnp.concatenate([row[:, None] * inv, col[:, None] * inv], -1)
    return jnp.cos(ang), jnp.sin(ang)


def apply_rope(x, cos, sin):
    half = x.shape[-1] // 2
    xf = x.astype(F32)
    x1, x2 = xf[..., :half], xf[..., half:]
    c, s = cos[:, None], sin[:, None]
    return jnp.concatenate([x1 * c - x2 * s, x1 * s + x2 * c], -1).astype(x.dtype)


def split_columns(z):
    idx, s = [], 0
    for w in IN_WIDTHS[:-1]:
        s += w
        idx.append(s)
    return jnp.split(z, idx, axis=-1)


def to_blocks(t):
    B, S = t.shape[:2]
    return jnp.swapaxes(t.reshape(B, S // Q_BLOCK, Q_BLOCK, *t.shape[2:]), 0, 1)


def from_blocks(t):
    nb, B, qb = t.shape[:3]
    return jnp.swapaxes(t, 0, 1).reshape(B, nb * qb, *t.shape[3:])


def sweep_queries(fn, q):
    return from_blocks(lax.map(fn, to_blocks(q)))


def rglru_coeffs(xc, w_a, b_a, w_x, b_x, lam):
    B, T, W = xc.shape
    xb = xc.reshape(B, T, LRU_BLOCKS, W // LRU_BLOCKS)
    r = jax.nn.sigmoid((jnp.einsum('btnc,ncd->btnd', xb, w_a) + b_a).astype(F32)).reshape(B, T, W)
    i = jax.nn.sigmoid((jnp.einsum('btnc,ncd->btnd', xb, w_x) + b_x).astype(F32)).reshape(B, T, W)
    log_a = -LRU_C * r * jax.nn.softplus(-lam.astype(F32))
    a = jnp.exp(log_a)
    mult = jnp.sqrt(-jnp.expm1(2.0 * log_a))
    return a, mult * i * xc.astype(F32)


def _combine(e1, e2):
    a1, b1 = e1
    a2, b2 = e2
    return a1 * a2, a2 * b1 + b2


def linear_scan(a, b, h0, reverse):
    a_cum, b_cum = lax.associative_scan(_combine, (a, b), axis=1, reverse=reverse)
    return b_cum + a_cum * h0[:, None]


def rglru_branch(x_lat, g_lat, x_ctx, g_ctx, with_ctx, conv_w, conv_b, w_a, b_a, w_x, b_x, lam):
    B = x_lat.shape[0]
    xl = dwconv(x_lat, conv_w, conv_b, LRU_CONV_LEFT)
    xc = dwconv(x_ctx, conv_w, conv_b, LRU_CONV_LEFT)
    h_lat, h_ctx = [], []
    for d, reverse in enumerate((False, True)):
        p = (w_a[d], b_a[d], w_x[d], b_x[d], lam[d])
        hc = linear_scan(*rglru_coeffs(xc, *p), jnp.zeros((B, LRU_WIDTH), F32), reverse)
        h_end = hc[:, 0] if reverse else hc[:, -1]
        h_lat.append(linear_scan(*rglru_coeffs(xl, *p), h_end, reverse))
        h_ctx.append(hc)
    y_lat = (h_lat[0] + h_lat[1]).astype(x_lat.dtype) * jax.nn.gelu(g_lat)
    y_ctx = (h_ctx[0] + h_ctx[1]).astype(x_ctx.dtype) * jax.nn.gelu(g_ctx) if with_ctx else None
    return y_lat, y_ctx


def diff_attend(q, k, v, lam):
    s = jnp.einsum('bqhmd,bkhmd->bhmqk', q, k).astype(F32) * DIFF_D ** -0.5
    p = jax.nn.softmax(s, axis=-1)
    w = p[:, :, 0] - lam * p[:, :, 1]
    return jnp.einsum('bhqk,bkhd->bqhd', w.astype(v.dtype), v)


def gqa_attend(q, k, v):
    s = jnp.einsum('bqhgd,bkhd->bhgqk', q, k).astype(F32) * q.shape[-1] ** -0.5
    p = jax.nn.softmax(s, axis=-1).astype(v.dtype)
    return jnp.einsum('bhgqk,bkhd->bqhgd', p, v)


def neighbourhood_attention(q, k, v, k_ctx, v_ctx, rpb):
    B, S, H, d = q.shape
    rows = S // GRID_W
    wr = min(NA_WIN_ROWS, rows)
    wc = NA_WIN_COLS
    scale = d ** -0.5
    qg = q.reshape(B, rows, GRID_W, H, d)
    kg = k.reshape(B, rows, GRID_W, H, d)
    vg = v.reshape(B, rows, GRID_W, H, d)
    row_start = jnp.clip(jnp.arange(rows) - wr // 2, 0, rows - wr)
    col_q = jnp.arange(GRID_W)
    col_idx = jnp.clip(col_q - wc // 2, 0, GRID_W - wc)[:, None] + jnp.arange(wc)
    bias_cols = rpb[:, :, col_idx - col_q[:, None] + (NA_WIN_COLS - 1)]

    def one_row(args):
        q_row, r = args
        r0 = row_start[r]
        k_win = lax.dynamic_slice_in_dim(kg, r0, wr, axis=1)[:, :, col_idx]
        v_win = lax.dynamic_slice_in_dim(vg, r0, wr, axis=1)[:, :, col_idx]
        row_off = r0 + jnp.arange(wr) - r + (NA_WIN_ROWS - 1)
        bias = jnp.transpose(bias_cols[:, row_off], (0, 2, 1, 3)).astype(F32)
        s_win = jnp.einsum('bqhd,brqwhd->bhqrw', q_row, k_win).astype(F32) * scale + bias
        s_ctx = jnp.einsum('bqhd,bchd->bhqc', q_row, k_ctx).astype(F32) * scale
        s = jnp.concatenate([s_win.reshape(B, H, GRID_W, wr * wc), s_ctx], -1)
        p = jax.nn.softmax(s, axis=-1).astype(v.dtype)
        p_win = p[..., :wr * wc].reshape(B, H, GRID_W, wr, wc)
        p_ctx = p[..., wr * wc:]
        return (jnp.einsum('bhqrw,brqwhd->bqhd', p_win, v_win)
                + jnp.einsum('bhqc,bchd->bqhd', p_ctx, v_ctx))

    out = lax.map(one_row, (jnp.swapaxes(qg, 0, 1), jnp.arange(rows)))
    return jnp.swapaxes(out, 0, 1).reshape(B, S, H, d)


def gated_merge(ys, gate_cols, w_branch, w_out, b_out):
    m = 0
    for j, y in enumerate(ys):
        g = jax.nn.sigmoid(gate_cols[..., j * D_MODEL:(j + 1) * D_MODEL])
        m = m + g * (y @ w_branch[j])
    return m @ w_out + b_out


def mixer_block(h, hc, with_ctx, lam_init, w_in, b_in, lru_conv_w, lru_conv_b, lru_w_a, lru_b_a,
                lru_w_x, lru_b_x, lru_lambda, diff_lambda, diff_subln, gqa_q_norm, gqa_k_norm,
                na_rpb, w_branch, w_out, b_out):
    B, S, _ = h.shape
    C = hc.shape[1]
    (a_x, a_g, d_q, d_k, d_v, g_q, g_k, g_v, n_q, n_k, n_v, gates) = split_columns(h @ w_in + b_in)
    (a_xc, a_gc, d_qc, d_kc, d_vc, g_qc, g_kc, g_vc, n_qc, n_kc, n_vc, gates_c) = split_columns(hc @ w_in + b_in)

    y_a, y_ac = rglru_branch(a_x, a_g, a_xc, a_gc, with_ctx, lru_conv_w, lru_conv_b,
                             lru_w_a, lru_b_a, lru_w_x, lru_b_x, lru_lambda)

    cos_d, sin_d = axial_rope(S, DIFF_D)
    lp = diff_lambda.astype(F32)
    lam = jnp.exp(jnp.sum(lp[0] * lp[1])) - jnp.exp(jnp.sum(lp[2] * lp[3])) + lam_init

    def diff_heads(t, T, rope):
        t = t.reshape(B, T, DIFF_HEADS * 2, DIFF_D)
        if rope:
            t = apply_rope(t, cos_d, sin_d)
        return t.reshape(B, T, DIFF_HEADS, 2, DIFF_D)

    dq, dk = diff_heads(d_q, S, True), diff_heads(d_k, S, True)
    dqc, dkc = diff_heads(d_qc, C, False), diff_heads(d_kc, C, False)
    dv = d_v.reshape(B, S, DIFF_HEADS, DIFF_V)
    dvc = d_vc.reshape(B, C, DIFF_HEADS, DIFF_V)
    dk_all = jnp.concatenate([dkc, dk], 1)
    dv_all = jnp.concatenate([dvc, dv], 1)
    o_b = sweep_queries(lambda qb: diff_attend(qb, dk_all, dv_all, lam), dq)
    y_b = (rms_norm(o_b, diff_subln) * (1.0 - lam_init)).reshape(B, S, -1)

    cos_g, sin_g = axial_rope(S, GQA_D)
    grp = GQA_Q_HEADS // GQA_KV_HEADS
    gq = apply_rope(rms_norm(g_q.reshape(B, S, GQA_Q_HEADS, GQA_D), gqa_q_norm), cos_g, sin_g)
    gk = apply_rope(rms_norm(g_k.reshape(B, S, GQA_KV_HEADS, GQA_D), gqa_k_norm), cos_g, sin_g)
    gqc = rms_norm(g_qc.reshape(B, C, GQA_Q_HEADS, GQA_D), gqa_q_norm)
    gkc = rms_norm(g_kc.reshape(B, C, GQA_KV_HEADS, GQA_D), gqa_k_norm)
    gv = g_v.reshape(B, S, GQA_KV_HEADS, GQA_D)
    gvc = g_vc.reshape(B, C, GQA_KV_HEADS, GQA_D)
    gk_all = jnp.concatenate([gkc, gk], 1)
    gv_all = jnp.concatenate([gvc, gv], 1)
    y_c = sweep_queries(lambda qb: gqa_attend(qb, gk_all, gv_all),
                        gq.reshape(B, S, GQA_KV_HEADS, grp, GQA_D)).reshape(B, S, -1)

    nqc = n_qc.reshape(B, C, NA_HEADS, NA_D)
    nkc = n_kc.reshape(B, C, NA_HEADS, NA_D)
    nvc = n_vc.reshape(B, C, NA_HEADS, NA_D)
    y_d = neighbourhood_attention(n_q.reshape(B, S, NA_HEADS, NA_D), n_k.reshape(B, S, NA_HEADS, NA_D),
                                  n_v.reshape(B, S, NA_HEADS, NA_D), nkc, nvc, na_rpb).reshape(B, S, -1)

    out = gated_merge((y_a, y_b, y_c, y_d), gates, w_branch, w_out, b_out)
    if not with_ctx:
        return out, None
    y_bc = (rms_norm(diff_attend(dqc, dkc, dvc, lam), diff_subln) * (1.0 - lam_init)).reshape(B, C, -1)
    y_cc = gqa_attend(gqc.reshape(B, C, GQA_KV_HEADS, grp, GQA_D), gkc, gvc).reshape(B, C, -1)
    y_dc = gqa_attend(nqc[:, :, :, None], nkc, nvc).reshape(B, C, -1)
    out_c = gated_merge((y_ac, y_bc, y_cc, y_dc), gates_c, w_branch, w_out, b_out)
    return out, out_c


def conv_ffn(h, w1, b1, conv_w, conv_b, w2, b2):
    u = dwconv(h @ w1 + b1, conv_w, conv_b, FFN_CONV // 2)
    val, gate = jnp.split(u, 2, axis=-1)
    return (jax.nn.silu(gate) * val) @ w2 + b2


def setup_inputs(seed: int = 0) -> dict:
    key = jax.random.key(seed)
    ks = iter(jax.random.split(key, 48))

    def nrm(shape, std):
        return std * jax.random.normal(next(ks), shape, F32)

    L, D = DEPTH, D_MODEL
    bw = LRU_WIDTH // LRU_BLOCKS
    u = jax.random.uniform(next(ks), (L, 2, LRU_WIDTH), F32, 0.9, 0.999)
    a_base = u ** (1.0 / LRU_C)
    lru_lambda = jnp.log(a_base) - jnp.log1p(-a_base)
    return {
        'x': nrm((BATCH, SEQ, D), 1.0),
        'c': nrm((BATCH, D), 1.0),
        'ctx': nrm((BATCH, CTX_LEN, D), 1.0),
        'c_ctx': nrm((D,), 1.0),
        'w_ada': nrm((L, D, 6 * D), 0.5 * D ** -0.5),
        'b_ada': nrm((L, 6 * D), 0.02),
        'w_in': nrm((L, D, D_IN), D ** -0.5),
        'b_in': nrm((L, D_IN), 0.02),
        'lru_conv_w': nrm((L, LRU_CONV, LRU_WIDTH), LRU_CONV ** -0.5),
        'lru_conv_b': nrm((L, LRU_WIDTH), 0.02),
        'lru_w_a': nrm((L, 2, LRU_BLOCKS, bw, bw), bw ** -0.5),
        'lru_b_a': nrm((L, 2, LRU_BLOCKS, bw), 0.02),
        'lru_w_x': nrm((L, 2, LRU_BLOCKS, bw, bw), bw ** -0.5),
        'lru_b_x': nrm((L, 2, LRU_BLOCKS, bw), 0.02),
        'lru_lambda': lru_lambda,
        'diff_lambda': nrm((L, 4, DIFF_D), 0.1),
        'diff_subln': 1.0 + nrm((L, DIFF_V), 0.02),
        'gqa_q_norm': 1.0 + nrm((L, GQA_D), 0.02),
        'gqa_k_norm': 1.0 + nrm((L, GQA_D), 0.02),
        'na_rpb': nrm((L, NA_HEADS, 2 * NA_WIN_ROWS - 1, 2 * NA_WIN_COLS - 1), 0.1),
        'w_branch': nrm((L, N_BRANCH, BRANCH_W, D), BRANCH_W ** -0.5),
        'w_out': nrm((L, D, D), BETA * D ** -0.5),
        'b_out': nrm((L, D), 0.02),
        'ln1_g': 1.0 + nrm((L, D), 0.02),
        'ln1_b': nrm((L, D), 0.02),
        'ffn_w1': nrm((L, D, 2 * D_FF), D ** -0.5),
        'ffn_b1': nrm((L, 2 * D_FF), 0.02),
        'ffn_conv_w': nrm((L, FFN_CONV, 2 * D_FF), FFN_CONV ** -0.5),
        'ffn_conv_b': nrm((L, 2 * D_FF), 0.02),
        'ffn_w2': nrm((L, D_FF, D), BETA * D_FF ** -0.5),
        'ffn_b2': nrm((L, D), 0.02),
        'ln2_g': 1.0 + nrm((L, D), 0.02),
        'ln2_b': nrm((L, D), 0.02),
    }


def reference(x, c, ctx, c_ctx, w_ada, b_ada, w_in, b_in, lru_conv_w, lru_conv_b, lru_w_a, lru_b_a,
              lru_w_x, lru_b_x, lru_lambda, diff_lambda, diff_subln, gqa_q_norm, gqa_k_norm, na_rpb,
              w_branch, w_out, b_out, ln1_g, ln1_b, ffn_w1, ffn_b1, ffn_conv_w, ffn_conv_b, ffn_w2,
              ffn_b2, ln2_g, ln2_b):
    xc = ctx
    silu_c = jax.nn.silu(c)
    silu_cc = jax.nn.silu(c_ctx)
    for l in range(DEPTH):
        with_ctx = l < DEPTH - 1
        lam_init = 0.8 - 0.6 * math.exp(-0.3 * l)
        mod = (silu_c @ w_ada[l] + b_ada[l])[:, None, :]
        mod_c = silu_cc @ w_ada[l] + b_ada[l]
        sh1, sc1, g1, sh2, sc2, g2 = jnp.split(mod, 6, axis=-1)
        sh1c, sc1c, g1c, sh2c, sc2c, g2c = jnp.split(mod_c, 6, axis=-1)

        m, mc = mixer_block(modulate(x, sh1, sc1), modulate(xc, sh1c, sc1c), with_ctx, lam_init,
                            w_in[l], b_in[l], lru_conv_w[l], lru_conv_b[l], lru_w_a[l], lru_b_a[l],
                            lru_w_x[l], lru_b_x[l], lru_lambda[l], diff_lambda[l], diff_subln[l],
                            gqa_q_norm[l], gqa_k_norm[l], na_rpb[l], w_branch[l], w_out[l], b_out[l])
        x = layer_norm(ALPHA * x + g1 * m, ln1_g[l], ln1_b[l])
        f = conv_ffn(modulate(x, sh2, sc2), ffn_w1[l], ffn_b1[l], ffn_conv_w[l], ffn_conv_b[l],
                     ffn_w2[l], ffn_b2[l])
        x = layer_norm(ALPHA * x + g2 * f, ln2_g[l], ln2_b[l])

        if with_ctx:
            xc = layer_norm(ALPHA * xc + g1c * mc, ln1_g[l], ln1_b[l])
            fc = conv_ffn(modulate(xc, sh2c, sc2c), ffn_w1[l], ffn_b1[l], ffn_conv_w[l], ffn_conv_b[l],
                          ffn_w2[l], ffn_b2[l])
            xc = layer_norm(ALPHA * xc + g2c * fc, ln2_g[l], ln2_b[l])
    return x
```

```python
from contextlib import ExitStack
import math
import numpy as np
import concourse.bass as bass
import concourse.mybir as mybir
from concourse.bass_utils import run_bass_kernel_spmd

F32 = mybir.dt.float32
BF16 = mybir.dt.bfloat16
AF = mybir.ActivationFunctionType
ALU = mybir.AluOpType
AX = mybir.AxisListType

D = 1024
SL = 2048
CT = 256
T = SL + CT
NT = T // 128
L = 4
DFF = 2816
NFC = DFF // 128
ALPHA = (2 * L) ** 0.25
EPS = 1e-6
TQ = [(0, 512), (512, 1024), (1024, 1536), (1536, 2048), (2048, 2304)]
NCORES = 8
NFM = 28
NTM = 640
GOFF = NFM * 128 + NTM
NCOLX = GOFF + 4096
EPOCH = 16000


class Sched:
    ENGS = ['pe', 'act', 'dve', 'pool', 'sp']

    def __init__(self, nc):
        self.nc = nc
        self.q = {e: [] for e in self.ENGS}
        self.n = {e: 0 for e in self.ENGS}
        self.last_w = {}
        self.readers = {}
        self.seen = {e: {} for e in self.ENGS}
        self.dma_cnt = {}
        self.nwaits = 0

    def _deps(self, eng, reads, writes):
        waits = {}
        seen = self.seen[eng]
        is_pe = eng == 'pe'

        def add(t):
            sk, val, teng = t
            if is_pe and teng == 'pe':
                return
            if seen.get(sk, 0) >= val:
                return
            if waits.get(sk, 0) < val:
                waits[sk] = val
        for r in reads:
            t = self.last_w.get(r)
            if t is not None:
                add(t)
        for w in writes:
            t = self.last_w.get(w)
            if t is not None:
                add(t)
            rd = self.readers.get(w)
            if rd:
                for sk, (val, teng) in rd.items():
                    add((sk, val, teng))
        for sk, val in waits.items():
            seen[sk] = val
        return list(waits.items())

    def _commit(self, tok, reads, writes):
        sk, val, teng = tok
        for r in reads:
            d = self.readers.get(r)
            if d is None:
                d = {}
                self.readers[r] = d
            d[sk] = (val, teng)
        for w in writes:
            self.last_w[w] = tok
            self.readers[w] = None

    def op(self, eng, fn, reads=(), writes=()):
        waits = self._deps(eng, reads, writes)
        n = self.n[eng]
        self.n[eng] = n + 1
        sk = 'E_%s_%d' % (eng, n // EPOCH)
        tok = (sk, n % EPOCH + 1, eng)
        self.q[eng].append((fn, waits, (sk, 1)))
        self.nwaits += len(waits)
        self._commit(tok, reads, writes)
        return tok

    def dma(self, eng, fn, key, reads=(), writes=()):
        waits = self._deps(eng, reads, writes)
        c = self.dma_cnt.get(key, 0)
        self.dma_cnt[key] = c + 1
        sk = 'D_%s_%d' % (key, c // 1000)
        tok = (sk, (c % 1000 + 1) * 16, 'dma')
        self.q[eng].append((fn, waits, (sk, 16)))
        self.nwaits += len(waits)
        self._commit(tok, reads, writes)
        return tok

    def final_wait(self, eng, toks):
        waits = {}
        for (sk, val, _) in toks:
            if waits.get(sk, 0) < val:
                waits[sk] = val
        self.q[eng].append((None, list(waits.items()), None))

    def emit(self, stack):
        nc = self.nc
        names = set()
        for e in self.ENGS:
            for (fn, waits, inc) in self.q[e]:
                for sk, _ in waits:
                    names.add(sk)
                if inc is not None:
                    names.add(inc[0])
        sems = {}
        for i, sk in enumerate(sorted(names)):
            sems[sk] = stack.enter_context(nc.semaphore('s%d' % i))
        self.nsems = len(sems)
        block = stack.enter_context(nc.Block())
        handles = {'pe': block.tensor, 'act': block.scalar, 'dve': block.vector,
                   'pool': block.gpsimd, 'sp': block.sync}
        for e in self.ENGS:
            ops = self.q[e]

            def body(eng, ops=ops):
                for (fn, waits, inc) in ops:
                    for sk, val in waits:
                        eng.wait_ge(sems[sk], val)
                    if fn is not None:
                        inst = fn(eng)
                        inst.then_inc(sems[inc[0]], inc[1])
            handles[e](body)


GRAN = 512
import os
KATTN = int(os.environ.get('KATTN', '9'))


class Region:
    def __init__(self, nc, st, name, nbytes, gran=GRAN):
        self.name = name
        self.nbytes = nbytes
        self.gran = gran
        self.t = st.enter_context(nc.sbuf_tensor(name, [128, nbytes // 4], F32))


class Buf:
    def __init__(self, reg, off, shape, dtype):
        self.reg = reg
        self.off = off
        self.shape = tuple(shape)
        self.esz = 4 if dtype == F32 else 2
        n = 1
        for s in shape:
            n *= s
        self.n = n
        nb = n * self.esz
        assert off % 4 == 0 and nb % 4 == 0 and off + nb <= reg.nbytes, (reg.name, off, nb, reg.nbytes)
        ap = reg.t[:, off // 4:(off + nb) // 4]
        if dtype != F32:
            ap = ap.bitcast(dtype)
        if len(shape) == 2:
            ap = ap.rearrange('p (a b) -> p a b', b=shape[1])
        elif len(shape) == 3:
            ap = ap.rearrange('p (a b c) -> p a b c', b=shape[1], c=shape[2])
        elif len(shape) == 4:
            ap = ap.rearrange('p (a b c d) -> p a b c d', b=shape[1], c=shape[2], d=shape[3])
        self.ap = ap
        self.end = off + nb

    def k(self, lo=0, hi=None):
        if hi is None:
            hi = self.n
        b0 = (self.off + lo * self.esz) // self.reg.gran
        b1 = (self.off + hi * self.esz - 1) // self.reg.gran
        nm = self.reg.name
        return [(nm, g) for g in range(b0, b1 + 1)]

    def k2(self, i, lo=0, hi=None):
        inner = self.n // self.shape[0]
        if hi is None:
            hi = inner
        return self.k(i * inner + lo, i * inner + hi)


OFF = dict(a_x=0, a_g=256, d_q=512, d_k=768, d_v=1024, g_q=1280, g_k=1536, g_v=1664,
           n_q=1792, n_k=2048, n_v=2304, gates=2560)


def _perm(base, n, grp):
    idx = np.arange(n)
    g = idx // grp
    j = idx % grp
    return base + g * grp + (j + grp // 2) % grp


def _col_index():
    ar = np.arange
    kd0 = np.concatenate([ar(64), ar(64)])
    kd1 = kd0 + 64
    p64 = (ar(64) + 32) % 64
    kp0 = np.concatenate([p64, p64])
    kp1 = kp0 + 64
    g3 = np.concatenate([np.concatenate([min(3 * c + s_, 7) * 32 + ar(32) for s_ in (0, 1, 2, 0)]) for c in range(3)])
    fm = np.concatenate([
        OFF['a_x'] + ar(256), OFF['a_g'] + ar(256),
        OFF['d_q'] + g3, _perm(OFF['d_q'], 256, 32)[g3],
        OFF['d_k'] + g3, _perm(OFF['d_k'], 256, 32)[g3],
        OFF['g_q'] + ar(256), _perm(OFF['g_q'], 256, 64),
        OFF['g_k'] + kd0, OFF['g_k'] + kd1, OFF['g_k'] + kp0, OFF['g_k'] + kp1,
        OFF['n_q'] + ar(256), OFF['n_k'] + ar(256)])
    assert fm.size == NFM * 128
    tm = np.concatenate([OFF['d_v'] + ar(256), OFF['g_v'] + ar(128), OFF['n_v'] + ar(256)])
    gate = np.concatenate([OFF['gates'] + j * 1024 + dc * 128 + ar(128) for dc in range(8) for j in range(4)])
    return fm, tm, gate


PFM_LAYOUT = [('bada', 48), ('binfm', NFM), ('bgate', 32), ('lcw', 8), ('lcb', 2), ('lba', 4), ('lbx', 4),
              ('llam', 4), ('gq', 1), ('gqp', 1), ('gk', 1), ('gkp', 1), ('fb1', 44), ('fcw', 132), ('fcb', 44)]
PFM_OFF = {}
_o = 0
for _n, _c in PFM_LAYOUT:
    PFM_OFF[_n] = _o
    _o += _c
NPF = _o
PROW_LAYOUT = [('binv', NTM), ('badag1', 1024), ('badag2', 1024), ('bout', 1024), ('fb2', 1024), ('ln1g', 1024),
               ('ln1b', 1024), ('ln2g', 1024), ('ln2b', 1024), ('subln', 64), ('dlam', 128)]
PROW_OFF = {}
_o = 0
for _n, _c in PROW_LAYOUT:
    PROW_OFF[_n] = _o
    _o += _c
NROW = _o


def _fm(v):
    v = np.asarray(v, np.float32).reshape(-1, 128)
    return np.ascontiguousarray(v.T)


def _rope_tables():
    t = np.arange(SL)
    row = (t // 64).astype(np.float32)
    col = (t % 64).astype(np.float32)
    out = np.zeros((4, 128, T), np.float32)
    for ti, dim in ((0, 32), (2, 64)):
        nf = dim // 4
        inv = (np.float32(10000.0) ** (-np.arange(nf, dtype=np.float32) / np.float32(nf))).astype(np.float32)
        ang = np.concatenate([row[:, None] * inv, col[:, None] * inv], -1).astype(np.float32)
        cos = np.cos(ang).astype(np.float32)
        sin = np.sin(ang).astype(np.float32)
        half = dim // 2
        for p in range(128):
            j = p % dim
            out[ti, p, :CT] = 1.0
            out[ti + 1, p, :CT] = 0.0
            out[ti, p, CT:] = cos[:, j % half]
            out[ti + 1, p, CT:] = -sin[:, j % half] if j < half else sin[:, j % half]
    return out


NA_HEAD_ORDER = [0, 2, 1, 3]
NA_VARIANT_R = [0, 2, 4, 28, 30]


def _na_sr(r):
    return min(max(r - 4, 0), 22)


def _na_variant(r):
    if r == 0:
        return 0
    if r == 2:
        return 1
    if r == 28:
        return 3
    if r == 30:
        return 4
    return 2


def _na_bias_tables(rpb):
    Ln = rpb.shape[0]
    out = np.full((Ln, 5, 128, 5, 4, 128), -30000.0, np.float32)
    q = np.arange(128)
    for v, r in enumerate(NA_VARIANT_R):
        sr = _na_sr(r)
        qr = r + q // 64
        qc = q % 64
        r0 = np.clip(qr - 4, 0, 24)
        c0 = np.clip(qc - 8, 0, 48)
        for kt in range(5):
            kk = np.arange(128)
            kr = sr + 2 * kt + kk // 64
            kc = kk % 64
            inwin = ((kr[:, None] >= r0[None, :]) & (kr[:, None] < r0[None, :] + 8) &
                     (kc[:, None] >= c0[None, :]) & (kc[:, None] < c0[None, :] + 16))
            ro = np.clip(kr[:, None] - qr[None, :] + 7, 0, 14)
            co = np.clip(kc[:, None] - qc[None, :] + 15, 0, 30)
            for slot, h in enumerate(NA_HEAD_ORDER):
                vals = rpb[:, h][:, ro, co]
                cur = out[:, v, :, kt, slot, :]
                out[:, v, :, kt, slot, :] = np.where(inwin[None], vals, cur)
    return out.reshape(Ln, 5, 128, 5 * 4 * 128)


def prep_inputs(inp):
    f32 = lambda a: np.ascontiguousarray(np.asarray(a, np.float32))
    fm, tm, gate = _col_index()
    colx = np.concatenate([fm, tm, gate])
    w_in = f32(inp['w_in'])
    b_in = f32(inp['b_in'])
    shared = {}
    shared['w_ada'] = f32(inp['w_ada'])
    shared['w_inx'] = np.ascontiguousarray(w_in[:, :, colx])
    wb = f32(inp['w_branch'])
    wb = wb.reshape(L, 4, 2, 128, 8, 128)
    shared['w_brx'] = np.ascontiguousarray(wb.transpose(0, 4, 3, 1, 2, 5)).reshape(L, 8, 128, 1024)
    shared['w_out'] = f32(inp['w_out'])
    w1 = f32(inp['ffn_w1']).reshape(L, D, 2, NFC, 128)
    shared['ffn_w1x'] = np.ascontiguousarray(w1.transpose(0, 1, 3, 2, 4)).reshape(L, D, 2 * DFF)
    shared['ffn_w2'] = f32(inp['ffn_w2'])
    wa = f32(inp['lru_w_a'])
    wx = f32(inp['lru_w_x'])
    wbd = np.zeros((L, 2, 2, 2, 128, 128), np.float32)
    for gi, wsrc in enumerate((wa, wx)):
        for d in range(2):
            for c in range(2):
                wbd[:, gi, d, c, 0:64, 0:64] = wsrc[:, d, 2 * c]
                wbd[:, gi, d, c, 64:128, 64:128] = wsrc[:, d, 2 * c + 1]
    shared['lru_wbd'] = wbd.reshape(L, 8, 128, 128)
    shared['na_bias'] = _na_bias_tables(f32(inp['na_rpb']))
    shared['rope'] = _rope_tables()
    shared['ident'] = np.eye(128, dtype=np.float32)
    pfm = np.zeros((L, 128, NPF), np.float32)
    prow = np.zeros((L, NROW), np.float32)
    gqn = f32(inp['gqa_q_norm'])
    gkn = f32(inp['gqa_k_norm'])
    p64 = (np.arange(64) + 32) % 64
    b1 = f32(inp['ffn_b1'])
    cw = f32(inp['ffn_conv_w'])
    cb = f32(inp['ffn_conv_b'])
    for l in range(L):
        def put(name, arr):
            o = PFM_OFF[name]
            pfm[l, :, o:o + arr.shape[1]] = arr
        put('bada', _fm(inp['b_ada'][l]))
        put('binfm', _fm(b_in[l][fm]))
        put('bgate', _fm(b_in[l][gate]))
        put('lcw', _fm(f32(inp['lru_conv_w'])[l].reshape(-1)))
        put('lcb', _fm(inp['lru_conv_b'][l]))
        put('lba', _fm(f32(inp['lru_b_a'])[l].reshape(-1)))
        put('lbx', _fm(f32(inp['lru_b_x'])[l].reshape(-1)))
        put('llam', _fm(f32(inp['lru_lambda'])[l].reshape(-1)))
        put('gq', _fm(np.tile(gqn[l], 2)))
        put('gqp', _fm(np.tile(gqn[l][p64], 2)))
        put('gk', _fm(np.tile(gkn[l], 2)))
        put('gkp', _fm(np.tile(gkn[l][p64], 2)))
        put('fb1', _fm(b1[l]))
        put('fcw', _fm(cw[l].reshape(-1)))
        put('fcb', _fm(cb[l]))

        def putr(name, arr):
            o = PROW_OFF[name]
            prow[l, o:o + arr.size] = np.asarray(arr, np.float32).reshape(-1)
        putr('binv', b_in[l][tm])
        putr('badag1', inp['b_ada'][l][2 * D:3 * D])
        putr('badag2', inp['b_ada'][l][5 * D:6 * D])
        putr('bout', inp['b_out'][l])
        putr('fb2', inp['ffn_b2'][l])
        putr('ln1g', inp['ln1_g'][l])
        putr('ln1b', inp['ln1_b'][l])
        putr('ln2g', inp['ln2_g'][l])
        putr('ln2b', inp['ln2_b'][l])
        putr('subln', inp['diff_subln'][l])
        putr('dlam', f32(inp['diff_lambda'])[l].reshape(-1))
    shared['pfm'] = pfm
    shared['prow'] = prow
    x = f32(inp['x'])
    ctx = f32(inp['ctx'])
    c = f32(inp['c'])
    cc = f32(inp['c_ctx'])
    per_core = []
    for b in range(x.shape[0]):
        cv = np.stack([c[b], cc], 0)
        cvT = np.ascontiguousarray(cv.reshape(2, 8, 128).transpose(2, 1, 0))
        per_core.append({'xin': np.concatenate([ctx[b], x[b]], 0), 'cvT': cvT})
    return shared, per_core


SHARED_SHAPES = dict(w_ada=[L, D, 6 * D], w_inx=[L, D, NCOLX], w_brx=[L, 8, 128, 1024], w_out=[L, D, D],
                     ffn_w1x=[L, D, 2 * DFF], ffn_w2=[L, DFF, D], lru_wbd=[L, 8, 128, 128],
                     na_bias=[L, 5, 128, 2560], rope=[4, 128, T], ident=[128, 128], pfm=[L, 128, NPF],
                     prow=[L, NROW], xin=[T, D], cvT=[128, 8, 2])


def build(nl=L, debug=False, stop=None):
    nc = bass.Bass('TRN2', target_bir_lowering=False)
    di = {n: nc.dram_tensor(n, list(s), F32, kind='ExternalInput').ap() for n, s in SHARED_SHAPES.items()}
    out_d = nc.dram_tensor('out', [SL, D], F32, kind='ExternalOutput').ap()
    skind = 'ExternalOutput' if debug else 'Internal'
    yT_d = nc.dram_tensor('yT_d', [8, 128, T], BF16, kind=skind).ap()
    mT_d = nc.dram_tensor('mT_d', [8, 128, T], BF16, kind=skind).ap()
    aT_d = nc.dram_tensor('aT_d', [NFC, 128, T], BF16, kind=skind).ap()
    dbg_d = None
    if debug:
        dbg_d = nc.dram_tensor('dbg', [T, D], F32, kind='ExternalOutput').ap()
        dbgh_d = nc.dram_tensor('dbgh', [8, 128, T], BF16, kind='ExternalOutput').ap()

    with ExitStack() as st:
        RX = Region(nc, st, 'RX', 72 * 1024)
        RH = Region(nc, st, 'RH', 36 * 1024)
        RW = Region(nc, st, 'RW', 32 * 1024)
        RA = Region(nc, st, 'RA', 54 * 1024)
        RC = Region(nc, st, 'RC', 13824, gran=64)
        PS = [st.enter_context(nc.psum_tensor('ps%d' % i, [128, 512], F32)) for i in range(8)]
        PSK = [('ps', i) for i in range(8)]
        S = Sched(nc)

        def mm(out, lhsT, rhs, start, stop_, r, w, skip=False):
            S.op('pe', lambda e: e.matmul(out, lhsT=lhsT, rhs=rhs, start=start, stop=stop_, skip_group_check=skip), r, w)

        def tr(out, in_, ident, r, w):
            S.op('pe', lambda e: e.transpose(out, in_, ident), r, w)

        def act(out, in_, func, r, w, bias=None, scale=None):
            kw = {}
            if bias is not None:
                kw['bias'] = bias
            if scale is not None:
                kw['scale'] = scale
            S.op('act', lambda e: e.activation(out=out, in_=in_, func=func, **kw), r, w)

        def ts(out, in0, s1, s2, op0, op1, r, w, eng='dve'):
            if op1 is None:
                S.op(eng, lambda e: e.tensor_scalar(out=out, in0=in0, scalar1=s1, scalar2=None, op0=op0), r, w)
            else:
                S.op(eng, lambda e: e.tensor_scalar(out=out, in0=in0, scalar1=s1, scalar2=s2, op0=op0, op1=op1), r, w)

        def tt(out, in0, in1, op, r, w, eng='dve'):
            S.op(eng, lambda e: e.tensor_tensor(out=out, in0=in0, in1=in1, op=op), r, w)

        def stt(out, in0, scalar, in1, op0, op1, r, w):
            S.op('dve', lambda e: e.scalar_tensor_tensor(out=out, in0=in0, scalar=scalar, in1=in1, op0=op0, op1=op1), r, w)

        def cp(out, in_, r, w, eng='dve'):
            S.op(eng, lambda e: e.tensor_copy(out=out, in_=in_), r, w)

        def mset(ap, val, w, eng='dve'):
            S.op(eng, lambda e: e.memset(ap, val), (), w)

        def ld(buf, out, in_, w=None, r=(), sub=0):
            key = 'L%s_%d_%d' % (buf.reg.name, buf.off, sub)
            if w is None:
                w = buf.k()
            if in_.dtype != out.dtype:
                return S.dma('pool', lambda e: e.dma_start(out=out, in_=in_, max_dma_last_dim=4096), key, r, w)
            return S.dma('pool', lambda e: e.dma_start(out=out, in_=in_), key, r, w)

        def stq(buf, out, in_, r=None, w=(), sub=0):
            key = 'S%s_%d_%d' % (buf.reg.name, buf.off, sub)
            if r is None:
                r = buf.k()
            return S.dma('sp', lambda e: e.dma_start(out=out, in_=in_), key, r, w)

        XS = Buf(RX, 0, (NT, D), F32)
        HT = Buf(RH, 0, (8, T), BF16)
        o = 0

        def rc(shape, dtype):
            nonlocal o
            b = Buf(RC, o, shape, dtype)
            o = (b.end + 63) // 64 * 64
            return b
        IDB = rc((128,), BF16)
        BONES = rc((128,), F32)
        PFM = rc((NPF,), F32)
        SILT = rc((8, 2), F32)
        SILB = rc((8, 2), BF16)
        SCB = rc((8, 2, 128), BF16)
        ONES = rc((128,), F32)
        MODF = rc((4, 8, 2), F32)
        STATS = rc((NT, 12), F32)
        MV = rc((NT, 2), F32)
        SD = rc((NT,), F32)
        RSTD = rc((NT,), F32)
        XN = [rc((D,), BF16) for _ in range(2)]
        LAMB = rc((8,), F32)
        LRUS = rc((16,), F32)
        SMALL = rc((64,), F32)
        EPSB = rc((2,), F32)

        ld(IDB, IDB.ap, di['ident'])
        mset(BONES.ap, 0.0, BONES.k())
        mset(BONES.ap[0:64, 0:64], 1.0, BONES.k())
        mset(BONES.ap[64:128, 64:128], 1.0, BONES.k())
        mset(ONES.ap, 1.0, ONES.k())
        mset(EPSB.ap[:, 0:1], EPS, EPSB.k())
        ld(SILT, SILT.ap, di['cvT'])
        act(SILT.ap, SILT.ap, AF.Silu, SILT.k(), SILT.k())
        cp(SILB.ap, SILT.ap, SILT.k(), SILB.k())
        for kc in range(8):
            for j in range(2):
                act(SCB.ap[:, kc, j, :], ONES.ap, AF.Identity, ONES.k() + SILT.k(), SCB.k(), scale=SILT.ap[:, kc, j:j + 1])
        xin_t = di['xin'].rearrange('(i p) d -> p i d', p=128)
        S.dma('sp', lambda e: e.dma_start(out=XS.ap, in_=xin_t), 'xin', (), XS.k())

        psr = [0]

        def psum(n=6):
            b = psr[0] % n
            psr[0] += 1
            return b

        def phase0_ada(l, groups):
            for g in groups:
                gi = {0: 0, 1: 1, 3: 2, 4: 3}[g]
                W = Buf(RW, 16384 * (gi % 2), (8, 1024), BF16)
                ld(W, W.ap, di['w_ada'][l, :, g * 1024:(g + 1) * 1024].rearrange('(kc p) n -> p kc n', p=128))
                for fc in range(8):
                    b = psum()
                    for kc in range(8):
                        mm(PS[b][:, 0:2], W.ap[:, kc, fc * 128:(fc + 1) * 128], SILB.ap[:, kc, :], kc == 0, kc == 7,
                           W.k2(kc, fc * 128, (fc + 1) * 128) + SILB.k(), [PSK[b]])
                    c0 = PFM_OFF['bada'] + g * 8 + fc
                    ts(MODF.ap[:, gi, fc, :], PS[b][:, 0:2], PFM.ap[:, c0:c0 + 1], 1.0 if g in (1, 4) else 0.0,
                       ALU.add, ALU.add, [PSK[b]] + PFM.k(), MODF.k())

        def ln_stats(i):
            for hf in range(2):
                S.op('dve', lambda e, hf=hf: e.bn_stats(out=STATS.ap[:, i, hf * 6:(hf + 1) * 6], in_=XS.ap[:, i, hf * 512:(hf + 1) * 512]),
                     XS.k2(i), STATS.k())
            S.op('dve', lambda e: e.bn_aggr(out=MV.ap[:, i, :], in_=STATS.ap[:, i, :]), STATS.k(), MV.k())
            act(SD.ap[:, i:i + 1], MV.ap[:, i, 1:2], AF.Sqrt, MV.k() + EPSB.k(), SD.k(), bias=EPSB.ap[:, 0:1], scale=1.0)
            S.op('dve', lambda e: e.reciprocal(out=RSTD.ap[:, i:i + 1], in_=SD.ap[:, i:i + 1]), SD.k(), RSTD.k())

        def ln_to_hT(i, gi):
            j = 1 if i < 2 else 0
            ln_stats(i)
            xn = XN[i % 2]
            ts(xn.ap, XS.ap[:, i, :], MV.ap[:, i, 0:1], RSTD.ap[:, i:i + 1], ALU.subtract, ALU.mult,
               XS.k2(i) + MV.k() + RSTD.k(), xn.k())
            b = 6 + (i % 2)
            pT = PS[b][:].bitcast(BF16).rearrange('p (a b) -> p a b', b=128)
            for kc in range(8):
                tr(pT[:, kc, :], xn.ap[:, kc * 128:(kc + 1) * 128], IDB.ap, xn.k() + IDB.k(), [PSK[b]])
            for kc in range(8):
                act(HT.ap[:, kc, i * 128:(i + 1) * 128], pT[:, kc, :], AF.Identity, [PSK[b]] + MODF.k(),
                    HT.k2(kc, i * 128, (i + 1) * 128), bias=MODF.ap[:, gi, kc, j:j + 1], scale=MODF.ap[:, gi + 1, kc, j:j + 1])

        def load_win(c0, ncols, off):
            W = Buf(RW, off, (8, ncols), BF16)
            ld(W, W.ap, di['w_inx'][l_cur[0], :, c0:c0 + ncols].rearrange('(kc p) n -> p kc n', p=128))
            return W

        def proj_fm(W, wc, tq, b):
            t0, t1 = TQ[tq]
            n = t1 - t0
            for kc in range(8):
                mm(PS[b][:, 0:n], W.ap[:, kc, wc * 128:(wc + 1) * 128], HT.ap[:, kc, t0:t1], kc == 0, kc == 7,
                   W.k2(kc, wc * 128, (wc + 1) * 128) + HT.k2(kc, t0, t1), [PSK[b]])
            return n

        def bcast_row(buf, name, n, key=None):
            o_ = PROW_OFF[name]
            ld(buf, buf.ap, di['prow'][l_cur[0], o_:o_ + n].partition_broadcast(128))

        l_cur = [0]

        def mixer_lru(l):
            XP = Buf(RA, 0, (2310,), F32)
            XC = Buf(RA, 9728, (T,), F32)
            XCB = Buf(RA, 19456, (T,), BF16)
            AA = Buf(RA, 24576, (T,), F32)
            BB = [Buf(RA, 34304, (T,), F32), Buf(RA, 44032, (T,), F32)]
            TM = [Buf(RW, 8192 + 2048 * i, (512,), F32) for i in range(4)]
            WBD = Buf(RW, 16384, (8, 128), BF16)
            YST = [Buf(RW, 18432 + 1024 * i, (512,), BF16) for i in range(2)]
            ld(WBD, WBD.ap, di['lru_wbd'][l].rearrange('g p n -> p g n'))
            pf = PFM.ap
            o_l = PFM_OFF['llam']
            act(LRUS.ap[:, 0:4], pf[:, o_l:o_l + 4], AF.Exp, PFM.k(), LRUS.k(), scale=-1.0)
            act(LRUS.ap[:, 0:4], LRUS.ap[:, 0:4], AF.Ln, LRUS.k(), LRUS.k(), bias=1.0, scale=1.0)
            ts(LRUS.ap[:, 4:8], LRUS.ap[:, 0:4], -8.0, None, ALU.mult, None, LRUS.k(), LRUS.k())
            ts(LRUS.ap[:, 8:12], LRUS.ap[:, 0:4], -16.0, None, ALU.mult, None, LRUS.k(), LRUS.k())
            Wx = load_win(0, 512, 0)
            for c in range(2):
                mset(XP.ap, 0.0, XP.k())
                bcol = PFM_OFF['binfm'] + c
                for tq in range(5):
                    b = psum()
                    n = proj_fm(Wx, c, tq, b)
                    t0, t1 = TQ[tq]
                    if tq == 0:
                        act(XP.ap[:, 2:258], PS[b][:, 0:256], AF.Identity, [PSK[b]] + PFM.k(), XP.k(), bias=pf[:, bcol:bcol + 1], scale=1.0)
                        act(XP.ap[:, 261:517], PS[b][:, 256:512], AF.Identity, [PSK[b]] + PFM.k(), XP.k(), bias=pf[:, bcol:bcol + 1], scale=1.0)
                    else:
                        act(XP.ap[:, t0 + 5:t1 + 5], PS[b][:, 0:n], AF.Identity, [PSK[b]] + PFM.k(), XP.k(), bias=pf[:, bcol:bcol + 1], scale=1.0)
                ow = PFM_OFF['lcw']
                ob = PFM_OFF['lcb'] + c
                for (x0, t0, n) in ((2, 0, 256), (261, 256, 2048)):
                    act(XC.ap[:, t0:t0 + n], XP.ap[:, x0 - 2:x0 - 2 + n], AF.Identity, XP.k() + PFM.k(), XC.k(t0, t0 + n),
                        bias=pf[:, ob:ob + 1], scale=pf[:, ow + c:ow + c + 1])
                    for k in range(1, 4):
                        stt(XC.ap[:, t0:t0 + n], XP.ap[:, x0 - 2 + k:x0 - 2 + k + n], pf[:, ow + 2 * k + c:ow + 2 * k + c + 1],
                            XC.ap[:, t0:t0 + n], ALU.mult, ALU.add, XP.k() + PFM.k() + XC.k(t0, t0 + n), XC.k(t0, t0 + n))
                cp(XCB.ap, XC.ap, XC.k(), XCB.k())
                for d in range(2):
                    ia = PFM_OFF['lba'] + d * 2 + c
                    ix = PFM_OFF['lbx'] + d * 2 + c
                    sc8 = LRUS.ap[:, 4 + d * 2 + c:5 + d * 2 + c]
                    sc16 = LRUS.ap[:, 8 + d * 2 + c:9 + d * 2 + c]
                    for tq in range(5):
                        t0, t1 = TQ[tq]
                        n = t1 - t0
                        b1 = psum()
                        mm(PS[b1][:, 0:n], WBD.ap[:, 0 * 4 + d * 2 + c, :], XCB.ap[:, t0:t1], True, True, WBD.k() + XCB.k(t0, t1), [PSK[b1]])
                        b2 = psum()
                        mm(PS[b2][:, 0:n], WBD.ap[:, 1 * 4 + d * 2 + c, :], XCB.ap[:, t0:t1], True, True, WBD.k() + XCB.k(t0, t1), [PSK[b2]])
                        r_, a2_, i_ = TM[0], TM[1], TM[2]
                        act(r_.ap[:, 0:n], PS[b1][:, 0:n], AF.Sigmoid, [PSK[b1]] + PFM.k(), r_.k(), bias=pf[:, ia:ia + 1], scale=1.0)
                        act(i_.ap[:, 0:n], PS[b2][:, 0:n], AF.Sigmoid, [PSK[b2]] + PFM.k(), i_.k(), bias=pf[:, ix:ix + 1], scale=1.0)
                        act(AA.ap[:, t0:t1], r_.ap[:, 0:n], AF.Exp, r_.k() + LRUS.k(), AA.k(t0, t1), scale=sc8)
                        act(a2_.ap[:, 0:n], r_.ap[:, 0:n], AF.Exp, r_.k() + LRUS.k(), a2_.k(), scale=sc16)
                        act(a2_.ap[:, 0:n], a2_.ap[:, 0:n], AF.Ln, a2_.k(), a2_.k(), bias=1.0, scale=-1.0)
                        act(a2_.ap[:, 0:n], a2_.ap[:, 0:n], AF.Exp, a2_.k(), a2_.k(), scale=0.5)
                        tt(i_.ap[:, 0:n], i_.ap[:, 0:n], a2_.ap[:, 0:n], ALU.mult, i_.k() + a2_.k(), i_.k())
                        tt(BB[d].ap[:, t0:t1], i_.ap[:, 0:n], XC.ap[:, t0:t1], ALU.mult, i_.k() + XC.k(t0, t1), BB[d].k(t0, t1))
                    Bd = BB[d]
                    if d == 0:
                        S.op('dve', lambda e, Bd=Bd: e.tensor_tensor_scan(out=Bd.ap[:, 0:CT], data0=AA.ap[:, 0:CT], data1=Bd.ap[:, 0:CT],
                                                                          initial=0.0, op0=ALU.mult, op1=ALU.add),
                             AA.k(0, CT) + Bd.k(0, CT), Bd.k(0, CT))
                        S.op('dve', lambda e, Bd=Bd: e.tensor_tensor_scan(out=Bd.ap[:, CT:T], data0=AA.ap[:, CT:T], data1=Bd.ap[:, CT:T],
                                                                          initial=Bd.ap[:, CT - 1:CT], op0=ALU.mult, op1=ALU.add),
                             AA.k() + Bd.k(), Bd.k(CT, T))
                    else:
                        S.op('dve', lambda e, Bd=Bd: e.tensor_tensor_scan(out=Bd.ap[:, 0:CT][:, ::-1], data0=AA.ap[:, 0:CT][:, ::-1],
                                                                          data1=Bd.ap[:, 0:CT][:, ::-1], initial=0.0, op0=ALU.mult, op1=ALU.add),
                             AA.k(0, CT) + Bd.k(0, CT), Bd.k(0, CT))
                        S.op('dve', lambda e, Bd=Bd: e.tensor_tensor_scan(out=Bd.ap[:, CT:T][:, ::-1], data0=AA.ap[:, CT:T][:, ::-1],
                                                                          data1=Bd.ap[:, CT:T][:, ::-1], initial=Bd.ap[:, 0:1],
                                                                          op0=ALU.mult, op1=ALU.add),
                             AA.k() + Bd.k(), Bd.k(CT, T))
                gcol = PFM_OFF['binfm'] + 2 + c
                for tq in range(5):
                    t0, t1 = TQ[tq]
                    b = psum()
                    n = proj_fm(Wx, 2 + c, tq, b)
                    g_, u_, s_ = TM[0], TM[1], TM[2]
                    act(g_.ap[:, 0:n], PS[b][:, 0:n], AF.Identity, [PSK[b]] + PFM.k(), g_.k(), bias=pf[:, gcol:gcol + 1], scale=1.0)
                    tt(u_.ap[:, 0:n], g_.ap[:, 0:n], g_.ap[:, 0:n], ALU.mult, g_.k(), u_.k())
                    ts(u_.ap[:, 0:n], u_.ap[:, 0:n], 0.044715, 1.0, ALU.mult, ALU.add, u_.k(), u_.k())
                    tt(u_.ap[:, 0:n], u_.ap[:, 0:n], g_.ap[:, 0:n], ALU.mult, u_.k() + g_.k(), u_.k())
                    act(s_.ap[:, 0:n], u_.ap[:, 0:n], AF.Sigmoid, u_.k(), s_.k(), scale=2.0 * math.sqrt(2.0 / math.pi))
                    tt(s_.ap[:, 0:n], s_.ap[:, 0:n], g_.ap[:, 0:n], ALU.mult, s_.k() + g_.k(), s_.k())
                    tt(u_.ap[:, 0:n], BB[0].ap[:, t0:t1], BB[1].ap[:, t0:t1], ALU.add, BB[0].k(t0, t1) + BB[1].k(t0, t1), u_.k())
                    ys = YST[tq % 2]
                    tt(ys.ap[:, 0:n], u_.ap[:, 0:n], s_.ap[:, 0:n], ALU.mult, u_.k() + s_.k(), ys.k())
                    stq(ys, yT_d[c, :, t0:t1], ys.ap[:, 0:n], w=[('yT', c, tq)])

        def v_proj(Wv, wc0, nh, VA, bvt, bo):
            ncol = nh * 64
            mset(VA.ap[:, :, :, 64:65], 1.0, VA.k())
            for i in range(NT):
                b = psum()
                for kc in range(8):
                    mm(PS[b][:, 0:ncol], HT.ap[:, kc, i * 128:(i + 1) * 128], Wv.ap[:, kc, wc0:wc0 + ncol], kc == 0, kc == 7,
                       HT.k2(kc, i * 128, (i + 1) * 128) + Wv.k2(kc, wc0, wc0 + ncol), [PSK[b]])
                tt(VA.ap[:, i, :, 0:64], PS[b][:, 0:ncol].rearrange('p (h d) -> p h d', d=64),
                   bvt.ap[:, bo:bo + ncol].rearrange('p (h d) -> p h d', d=64), ALU.add, [PSK[b]] + bvt.k(), VA.k2(i))

        def y_out(ybuf, branch, q0, nsub, YT2):
            for kc in range(2):
                b = 6 + (psr[0] % 2)
                psr[0] += 1
                pT = PS[b][:].bitcast(BF16)
                for j in range(nsub):
                    tr(pT[:, j * 128:(j + 1) * 128], ybuf.ap[:, j, kc * 128:(kc + 1) * 128], IDB.ap, ybuf.k() + IDB.k(), [PSK[b]])
                yt = YT2[kc]
                act(yt.ap[:, 0:nsub * 128], pT[:, 0:nsub * 128], AF.Copy, [PSK[b]], yt.k())
                stq(yt, yT_d[branch * 2 + kc, :, q0:q0 + nsub * 128], yt.ap[:, 0:nsub * 128],
                    w=[('yT', branch * 2 + kc, q0 // 512)])

        QCH = [(0, 2)] + [(2 + 4 * i, 4) for i in range(4)]

        def attn_core(QT, KT, VA, hmlist, scale, PTB, on_done):
            for qc, (qt0, nsub) in enumerate(QCH):
                q0 = qt0 * 128
                nq = nsub * 128
                ktiles = [0, 1] if qc == 0 else list(range(NT))
                for idx, (cq, ck, pb, K, vh) in enumerate(hmlist):
                    ob = 3 + (idx % 3)
                    Ov = PS[ob][:, 0:260].rearrange('p (j d) -> p j d', d=65)
                    for ki, kt in enumerate(ktiles):
                        sb = psum(3)
                        mm(PS[sb][:, 0:nq], KT.ap[pb:pb + K, ck, kt * 128:(kt + 1) * 128], QT.ap[pb:pb + K, cq, q0:q0 + nq], True, True,
                           KT.k2(ck, kt * 128, (kt + 1) * 128) + QT.k2(cq, q0, q0 + nq), [PSK[sb]])
                        pt = PTB[ptr[0] % len(PTB)]
                        ptr[0] += 1
                        act(pt.ap[:, 0:nq], PS[sb][:, 0:nq], AF.Exp, [PSK[sb]], pt.k(), scale=scale)
                        if KATTN < 2:
                            continue
                        for j in range(nsub):
                            mm(Ov[:, j, :], pt.ap[:, j * 128:(j + 1) * 128], VA.ap[:, kt, vh, :], ki == 0 and j == 0, ki == len(ktiles) - 1,
                               pt.k() + VA.k2(kt), [PSK[ob]], skip=True)
                    if KATTN >= 3:
                        on_done(qc, idx, ob, Ov, q0, nsub)

        ptr = [0]

        def mixer_diff(l, lam_init):
            QT = Buf(RA, 0, (3, T), BF16)
            KT = Buf(RA, 13824, (3, T), BF16)
            VA = Buf(RA, 27648, (NT, 4, 65), BF16)
            BV = Buf(RA, 37120, (256,), F32)
            SUB = Buf(RA, 38144, (64,), F32)
            DL = Buf(RA, 38656, (128,), F32)
            TB = [[Buf(RA, 39168 + 4096 * i + 2048 * s, (512,), F32) for s in range(2)] for i in range(2)]
            TMP = [Buf(RA, 47360 + 2048 * i, (512,), F32) for i in range(2)]
            PTB = [Buf(RA, 39168 + 1024 * i, (512,), BF16) for i in range(4)]
            OB = Buf(RA, 43264, (4, 256), F32)
            YB = Buf(RA, 47360, (4, 256), BF16)
            SM = Buf(RA, 49408, (64,), F32)
            T2 = Buf(RA, 49920, (4, 64), F32)
            YT2 = [Buf(RW, 28672 + 1024 * i, (512,), BF16) for i in range(2)]
            pf = PFM.ap
            bcast_row(DL, 'dlam', 128)
            bcast_row(SUB, 'subln', 64)
            bcast_row(BV, 'binv', 256)
            ts(SUB.ap, SUB.ap, 1.0 - lam_init, None, ALU.mult, None, SUB.k(), SUB.k())
            for z in range(2):
                tt(SM.ap[:, 0:32], DL.ap[:, z * 64:z * 64 + 32], DL.ap[:, z * 64 + 32:z * 64 + 64], ALU.mult, DL.k() + SM.k(), SM.k())
                S.op('dve', lambda e, z=z: e.tensor_reduce(out=LAMB.ap[:, z:z + 1], in_=SM.ap[:, 0:32], axis=AX.X, op=ALU.add),
                     SM.k(), LAMB.k())
            act(LAMB.ap[:, 0:2], LAMB.ap[:, 0:2], AF.Exp, LAMB.k(), LAMB.k())
            tt(LAMB.ap[:, 2:3], LAMB.ap[:, 0:1], LAMB.ap[:, 1:2], ALU.subtract, LAMB.k(), LAMB.k())
            ts(LAMB.ap[:, 3:4], LAMB.ap[:, 2:3], lam_init, -1.0, ALU.add, ALU.mult, LAMB.k(), LAMB.k())
            if stop == 'diff0':
                return
            Wq = load_win(4 * 128, 768, 0)
            Wk = load_win(10 * 128, 768, 12288)
            Wv = load_win(NFM * 128, 256, 24576)
            for tq in range(5):
                t0, t1 = TQ[tq]
                n = t1 - t0
                tb = TB[tq % 2]
                ld(tb[0], tb[0].ap[:, 0:n], di['rope'][0, :, t0:t1])
                ld(tb[1], tb[1].ap[:, 0:n], di['rope'][1, :, t0:t1])
                for (W, dst, cbase) in ((Wq, QT, 4), (Wk, KT, 10)):
                    for c in range(3):
                        b1 = psum()
                        proj_fm(W, c, tq, b1)
                        b2 = psum()
                        proj_fm(W, 3 + c, tq, b2)
                        bc = PFM_OFF['binfm'] + cbase + c
                        bp = PFM_OFF['binfm'] + cbase + 3 + c
                        stt(TMP[0].ap[:, 0:n], PS[b1][:, 0:n], pf[:, bc:bc + 1], tb[0].ap[:, 0:n], ALU.add, ALU.mult,
                            [PSK[b1]] + PFM.k() + tb[0].k(), TMP[0].k())
                        stt(TMP[1].ap[:, 0:n], PS[b2][:, 0:n], pf[:, bp:bp + 1], tb[1].ap[:, 0:n], ALU.add, ALU.mult,
                            [PSK[b2]] + PFM.k() + tb[1].k(), TMP[1].k())
                        tt(dst.ap[:, c, t0:t1], TMP[0].ap[:, 0:n], TMP[1].ap[:, 0:n], ALU.add, TMP[0].k() + TMP[1].k(), dst.k2(c, t0, t1))
            if stop == 'diff1':
                return
            v_proj(Wv, 0, 4, VA, BV, 0)
            if stop == 'diff2':
                return
            hml = []
            for h in range(4):
                for m in range(2):
                    hm = h * 2 + m
                    hml.append((hm // 3, hm // 3, (hm % 3) * 32, 32, h))

            def done(qc, idx, ob, Ov, q0, nsub):
                h, m = idx // 2, idx % 2
                if m == 0:
                    done.o1 = (ob, Ov)
                    return
                ob1, Ov1 = done.o1
                S.op('dve', lambda e: e.reciprocal(out=SM.ap[:, 0:nsub], in_=Ov1[:, 0:nsub, 64]), [PSK[ob1]], SM.k())
                S.op('dve', lambda e: e.reciprocal(out=SM.ap[:, 4:4 + nsub], in_=Ov[:, 0:nsub, 64]), [PSK[ob]], SM.k())
                ts(SM.ap[:, 4:4 + nsub], SM.ap[:, 4:4 + nsub], LAMB.ap[:, 3:4], None, ALU.mult, None, SM.k() + LAMB.k(), SM.k())
                for j in range(nsub):
                    ts(T2.ap[:, j, :], Ov[:, j, 0:64], SM.ap[:, 4 + j:5 + j], None, ALU.mult, None, [PSK[ob]] + SM.k(), T2.k())
                    stt(OB.ap[:, j, h * 64:(h + 1) * 64], Ov1[:, j, 0:64], SM.ap[:, j:j + 1], T2.ap[:, j, :], ALU.mult, ALU.add,
                        [PSK[ob1]] + SM.k() + T2.k(), OB.k())
                if h == 3:
                    for j in range(nsub):
                        tt(T2.ap[:, :, :], OB.ap[:, j, :].rearrange('p (h d) -> p h d', d=64),
                           OB.ap[:, j, :].rearrange('p (h d) -> p h d', d=64), ALU.mult, OB.k(), T2.k())
                        S.op('dve', lambda e, j=j: e.tensor_reduce(out=SM.ap[:, 8 + 4 * j:12 + 4 * j], in_=T2.ap[:, :, :], axis=AX.X, op=ALU.add),
                             T2.k(), SM.k())
                    act(SM.ap[:, 8:8 + 4 * nsub], SM.ap[:, 8:8 + 4 * nsub], AF.Sqrt, SM.k() + EPSB.k(), SM.k(), bias=EPSB.ap[:, 0:1], scale=1.0 / 64)
                    S.op('dve', lambda e: e.reciprocal(out=SM.ap[:, 8:8 + 4 * nsub], in_=SM.ap[:, 8:8 + 4 * nsub]), SM.k(), SM.k())
                    for j in range(nsub):
                        for hh in range(4):
                            stt(YB.ap[:, j, hh * 64:(hh + 1) * 64], OB.ap[:, j, hh * 64:(hh + 1) * 64], SM.ap[:, 8 + 4 * j + hh:9 + 4 * j + hh],
                                SUB.ap, ALU.mult, ALU.mult, OB.k() + SM.k() + SUB.k(), YB.k())
                    if KATTN >= 4:
                        y_out(YB, 1, q0, nsub, YT2)
            attn_core(QT, KT, VA, hml, 32 ** -0.5, PTB, done)

        def mixer_gqa(l):
            QT = Buf(RA, 0, (2, T), BF16)
            KT = Buf(RA, 9216, (2, T), BF16)
            VA = Buf(RA, 18432, (NT, 2, 65), BF16)
            BV = Buf(RA, 23552, (128,), F32)
            TB = [[Buf(RA, 24064 + 4096 * i + 2048 * s, (512,), F32) for s in range(2)] for i in range(2)]
            TMP = [Buf(RA, 32256 + 2048 * i, (512,), F32) for i in range(5)]
            PTB = [Buf(RA, 42496 + 1024 * i, (512,), BF16) for i in range(4)]
            YB = Buf(RA, 46592, (4, 256), BF16)
            SM = Buf(RA, 48640, (64,), F32)
            YT2 = [Buf(RW, 20480 + 1024 * i, (512,), BF16) for i in range(2)]
            pf = PFM.ap
            o_ = PROW_OFF['binv'] + 256
            ld(BV, BV.ap, di['prow'][l, o_:o_ + 128].partition_broadcast(128))
            Wq = load_win(16 * 128, 512, 0)
            Wk = load_win(20 * 128, 512, 8192)
            Wv = load_win(NFM * 128 + 256, 128, 16384)
            for tq in range(5):
                t0, t1 = TQ[tq]
                n = t1 - t0
                tb = TB[tq % 2]
                ld(tb[0], tb[0].ap[:, 0:n], di['rope'][2, :, t0:t1])
                ld(tb[1], tb[1].ap[:, 0:n], di['rope'][3, :, t0:t1])
                for (W, dst, cbase, gn, gpn) in ((Wq, QT, 16, 'gq', 'gqp'), (Wk, KT, 20, 'gk', 'gkp')):
                    for c in range(2):
                        b1 = psum()
                        proj_fm(W, c, tq, b1)
                        b2 = psum()
                        proj_fm(W, 2 + c, tq, b2)
                        bc = PFM_OFF['binfm'] + cbase + c
                        bp = PFM_OFF['binfm'] + cbase + 2 + c
                        zb, zpb, sq, t1_, t2_ = TMP
                        act(zb.ap[:, 0:n], PS[b1][:, 0:n], AF.Identity, [PSK[b1]] + PFM.k(), zb.k(), bias=pf[:, bc:bc + 1], scale=1.0)
                        act(sq.ap[:, 0:n], PS[b1][:, 0:n], AF.Square, [PSK[b1]] + PFM.k(), sq.k(), bias=pf[:, bc:bc + 1], scale=1.0)
                        act(zpb.ap[:, 0:n], PS[b2][:, 0:n], AF.Identity, [PSK[b2]] + PFM.k(), zpb.k(), bias=pf[:, bp:bp + 1], scale=1.0)
                        b3 = psum()
                        mm(PS[b3][:, 0:n], BONES.ap, sq.ap[:, 0:n], True, True, BONES.k() + sq.k(), [PSK[b3]])
                        act(sq.ap[:, 0:n], PS[b3][:, 0:n], AF.Ln, [PSK[b3]] + EPSB.k(), sq.k(), bias=EPSB.ap[:, 0:1], scale=1.0 / 64)
                        act(sq.ap[:, 0:n], sq.ap[:, 0:n], AF.Exp, sq.k(), sq.k(), scale=-0.5)
                        g1 = PFM_OFF[gn]
                        g2 = PFM_OFF[gpn]
                        stt(t1_.ap[:, 0:n], zb.ap[:, 0:n], pf[:, g1:g1 + 1], tb[0].ap[:, 0:n], ALU.mult, ALU.mult,
                            zb.k() + PFM.k() + tb[0].k(), t1_.k())
                        stt(t2_.ap[:, 0:n], zpb.ap[:, 0:n], pf[:, g2:g2 + 1], tb[1].ap[:, 0:n], ALU.mult, ALU.mult,
                            zpb.k() + PFM.k() + tb[1].k(), t2_.k())
                        tt(t1_.ap[:, 0:n], t1_.ap[:, 0:n], t2_.ap[:, 0:n], ALU.add, t1_.k() + t2_.k(), t1_.k())
                        tt(dst.ap[:, c, t0:t1], t1_.ap[:, 0:n], sq.ap[:, 0:n], ALU.mult, t1_.k() + sq.k(), dst.k2(c, t0, t1))
            v_proj(Wv, 0, 2, VA, BV, 0)
            hml = []
            for qh in range(4):
                hml.append((qh // 2, qh // 2, (qh % 2) * 64, 64, qh // 2))

            def done(qc, idx, ob, Ov, q0, nsub):
                S.op('dve', lambda e: e.reciprocal(out=SM.ap[:, 0:nsub], in_=Ov[:, 0:nsub, 64]), [PSK[ob]], SM.k())
                for j in range(nsub):
                    ts(YB.ap[:, j, idx * 64:(idx + 1) * 64], Ov[:, j, 0:64], SM.ap[:, j:j + 1], None, ALU.mult, None,
                       [PSK[ob]] + SM.k(), YB.k())
                if idx == 3:
                    y_out(YB, 2, q0, nsub, YT2)
            attn_core(QT, KT, VA, hml, 0.125, PTB, done)

        def mixer_na(l):
            QT = Buf(RA, 0, (2, T), BF16)
            KT = Buf(RA, 9216, (2, T), BF16)
            VA = Buf(RA, 18432, (NT, 4, 65), BF16)
            BV = Buf(RA, 27904, (256,), F32)
            NB = [Buf(RA, 28928 + 5120 * i, (5, 512), BF16) for i in range(2)]
            PTB = [Buf(RA, 39168 + 1024 * i, (512,), BF16) for i in range(4)]
            YB = Buf(RA, 43264, (4, 256), BF16)
            SM = Buf(RA, 45312, (64,), F32)
            YT2 = [Buf(RW, 20480 + 1024 * i, (512,), BF16) for i in range(2)]
            pf = PFM.ap
            o_ = PROW_OFF['binv'] + 384
            ld(BV, BV.ap, di['prow'][l, o_:o_ + 256].partition_broadcast(128))
            Wqk = load_win(24 * 128, 512, 0)
            Wv = load_win(NFM * 128 + 384, 256, 8192)
            for tq in range(5):
                t0, t1 = TQ[tq]
                for c in range(2):
                    b1 = psum()
                    n = proj_fm(Wqk, c, tq, b1)
                    bc = PFM_OFF['binfm'] + 24 + c
                    ts(QT.ap[:, c, t0:t1], PS[b1][:, 0:n], pf[:, bc:bc + 1], 0.125, ALU.add, ALU.mult, [PSK[b1]] + PFM.k(), QT.k2(c, t0, t1))
                    b2 = psum()
                    proj_fm(Wqk, 2 + c, tq, b2)
                    bk = PFM_OFF['binfm'] + 26 + c
                    act(KT.ap[:, c, t0:t1], PS[b2][:, 0:n], AF.Identity, [PSK[b2]] + PFM.k(), KT.k2(c, t0, t1), bias=pf[:, bk:bk + 1], scale=1.0)
            v_proj(Wv, 0, 4, VA, BV, 0)
            cur_v = -1
            nbi = 0
            for qt in range(NT):
                q0 = qt * 128
                if qt < 2:
                    ktl = [(0, None), (1, None)]
                else:
                    r = 2 * (qt - 2)
                    v = _na_variant(r)
                    if v != cur_v:
                        nbi += 1
                        ld(NB[nbi % 2], NB[nbi % 2].ap, di['na_bias'][l, v].rearrange('p (t n) -> p t n', n=512))
                        cur_v = v
                    nb = NB[nbi % 2]
                    sr = _na_sr(r)
                    ktl = [(0, None), (1, None)] + [(2 + sr // 2 + i, i) for i in range(5)]
                ob = 3 + (qt % 3)
                Ov = PS[ob][:, 0:260].rearrange('p (j d) -> p j d', d=65)
                for ki, (kt, bi) in enumerate(ktl):
                    pt = PTB[ptr[0] % 4]
                    ptr[0] += 1
                    for g in range(2):
                        sb = psum(3)
                        if bi is not None:
                            mm(PS[sb][:, 0:256], IDB.ap, nb.ap[:, bi, g * 256:(g + 1) * 256], True, False, IDB.k() + nb.k2(bi), [PSK[sb]])
                        for s2 in range(2):
                            h = NA_HEAD_ORDER[g * 2 + s2]
                            c, pb = h // 2, (h % 2) * 64
                            mm(PS[sb][:, s2 * 128:(s2 + 1) * 128], KT.ap[pb:pb + 64, c, kt * 128:(kt + 1) * 128],
                               QT.ap[pb:pb + 64, c, q0:q0 + 128], bi is None, (bi is None) or s2 == 1,
                               KT.k2(c, kt * 128, (kt + 1) * 128) + QT.k2(c, q0, q0 + 128), [PSK[sb]])
                        act(pt.ap[:, g * 256:(g + 1) * 256], PS[sb][:, 0:256], AF.Exp, [PSK[sb]], pt.k())
                    for h in range(4):
                        slot = NA_HEAD_ORDER.index(h)
                        mm(Ov[:, h, :], pt.ap[:, slot * 128:(slot + 1) * 128], VA.ap[:, kt, h, :], ki == 0 and h == 0, ki == len(ktl) - 1,
                           pt.k() + VA.k2(kt), [PSK[ob]], skip=True)
                S.op('dve', lambda e, Ov=Ov: e.reciprocal(out=SM.ap[:, 0:4], in_=Ov[:, 0:4, 64]), [PSK[ob]], SM.k())
                j = qt % 4 if qt >= 2 else qt
                for h in range(4):
                    ts(YB.ap[:, (qt - 2) % 4 if qt >= 2 else qt, h * 64:(h + 1) * 64], Ov[:, h, 0:64], SM.ap[:, h:h + 1], None, ALU.mult, None,
                       [PSK[ob]] + SM.k(), YB.k())
                if qt == 1:
                    y_out(YB, 3, 0, 2, YT2)
                elif qt >= 2 and (qt - 2) % 4 == 3:
                    y_out(YB, 3, (qt - 3) * 128, 4, YT2)

        def merge(l):
            YTA = Buf(RA, 0, (8, T), BF16)
            MT = Buf(RA, 36864, (T,), BF16)
            SG = [Buf(RA, 41472 + 2048 * i, (512,), F32) for i in range(2)]
            MA = [Buf(RA, 45568 + 2048 * i, (512,), F32) for i in range(2)]
            TP = Buf(RA, 49664, (512,), F32)
            pf = PFM.ap
            ld(YTA, YTA.ap, yT_d.rearrange('c p t -> p c t'), r=[('yT', ch, tq) for ch in range(8) for tq in range(5)])
            for dc in range(8):
                Wg = load_win(GOFF + dc * 512, 512, 8192 * (dc % 2))
                WB = Buf(RW, 16384 + 2048 * (dc % 2), (4, 2, 128), BF16)
                ld(WB, WB.ap, di['w_brx'][l, dc].rearrange('p (j k n) -> p j k n', j=4, k=2))
                for tq in range(5):
                    t0, t1 = TQ[tq]
                    n = t1 - t0
                    ma = MA[tq % 2]
                    for j in range(4):
                        bg = psum(3)
                        proj_fm(Wg, j, tq, bg)
                        by = 3 + psum(3)
                        for k2 in range(2):
                            mm(PS[by][:, 0:n], WB.ap[:, j, k2, :], YTA.ap[:, j * 2 + k2, t0:t1], k2 == 0, k2 == 1,
                               WB.k() + YTA.k2(j * 2 + k2, t0, t1), [PSK[by]])
                        sg = SG[j % 2]
                        bc = PFM_OFF['bgate'] + dc * 4 + j
                        act(sg.ap[:, 0:n], PS[bg][:, 0:n], AF.Sigmoid, [PSK[bg]] + PFM.k(), sg.k(), bias=pf[:, bc:bc + 1], scale=1.0)
                        if j == 0:
                            tt(ma.ap[:, 0:n], sg.ap[:, 0:n], PS[by][:, 0:n], ALU.mult, sg.k() + [PSK[by]], ma.k())
                        else:
                            tt(TP.ap[:, 0:n], sg.ap[:, 0:n], PS[by][:, 0:n], ALU.mult, sg.k() + [PSK[by]], TP.k())
                            if j < 3:
                                tt(ma.ap[:, 0:n], ma.ap[:, 0:n], TP.ap[:, 0:n], ALU.add, ma.k() + TP.k(), ma.k())
                            else:
                                tt(MT.ap[:, t0:t1], ma.ap[:, 0:n], TP.ap[:, 0:n], ALU.add, ma.k() + TP.k(), MT.k(t0, t1))
                stq(MT, mT_d[dc], MT.ap, w=[('mT', dc)])

        def make_gate_tiles(l, g, bname, badaname, base, woff):
            G = [Buf(base[0], base[1] + 4096 * j, (D,), F32) for j in range(2)]
            GB = [Buf(base[0], base[1] + 8192 + 4096 * j, (D,), F32) for j in range(2)]
            LNG = Buf(base[0], base[1] + 16384, (D,), F32)
            LNB = Buf(base[0], base[1] + 20480, (D,), F32)
            W = Buf(RW, woff, (8, 1024), BF16)
            ld(W, W.ap, di['w_ada'][l, :, g * 1024:(g + 1) * 1024].rearrange('(kc p) n -> p kc n', p=128))
            bcast_row(LNG, badaname, 1024)
            bcast_row(LNB, bname, 1024)
            for j in range(2):
                for hf in range(2):
                    b = psum()
                    for kc in range(8):
                        mm(PS[b][:, 0:512], SCB.ap[:, kc, j, :], W.ap[:, kc, hf * 512:(hf + 1) * 512], kc == 0, kc == 7,
                           SCB.k() + W.k2(kc, hf * 512, (hf + 1) * 512), [PSK[b]])
                    tt(G[j].ap[:, hf * 512:(hf + 1) * 512], PS[b][:, 0:512], LNG.ap[:, hf * 512:(hf + 1) * 512], ALU.add,
                       [PSK[b]] + LNG.k(hf * 512, (hf + 1) * 512), G[j].k(hf * 512, (hf + 1) * 512))
                tt(GB[j].ap, G[j].ap, LNB.ap, ALU.mult, G[j].k() + LNB.k(), GB[j].k())
            return G, GB, LNG, LNB

        def post_tile(i, pairs, G, GB, LNG, LNB, TMPB):
            j = 1 if i < 2 else 0
            for (b, hf) in pairs:
                sl = slice(hf * 512, (hf + 1) * 512)
                tb = TMPB[hf]
                tt(tb.ap, PS[b][:, 0:512], G[j].ap[:, sl], ALU.mult, [PSK[b]] + G[j].k(hf * 512, (hf + 1) * 512), tb.k())
                stt(XS.ap[:, i, sl], XS.ap[:, i, sl], ALPHA, tb.ap, ALU.mult, ALU.add, XS.k2(i, hf * 512, (hf + 1) * 512) + tb.k(),
                    XS.k2(i, hf * 512, (hf + 1) * 512))
            tt(XS.ap[:, i, :], XS.ap[:, i, :], GB[j].ap, ALU.add, XS.k2(i) + GB[j].k(), XS.k2(i))
            ln_stats(i)
            ts(XS.ap[:, i, :], XS.ap[:, i, :], MV.ap[:, i, 0:1], RSTD.ap[:, i:i + 1], ALU.subtract, ALU.mult,
               XS.k2(i) + MV.k() + RSTD.k(), XS.k2(i))
            tt(XS.ap[:, i, :], XS.ap[:, i, :], LNG.ap, ALU.mult, XS.k2(i) + LNG.k(), XS.k2(i))
            tt(XS.ap[:, i, :], XS.ap[:, i, :], LNB.ap, ALU.add, XS.k2(i) + LNB.k(), XS.k2(i))

        def wout_phase(l):
            G, GB, LNG, LNB = make_gate_tiles(l, 2, 'bout', 'badag1', (RA, 0), 16384)
            bcast_row(LNG, 'ln1g', 1024)
            bcast_row(LNB, 'ln1b', 1024)
            TMPB = [Buf(RA, 24576 + 2048 * i, (512,), F32) for i in range(2)]
            WO = Buf(RW, 0, (8, D), BF16)
            ld(WO, WO.ap, di['w_out'][l].rearrange('(kc p) n -> p kc n', p=128))
            MB = [Buf(RW, 16384 + 8192 * i, (8, 512), BF16) for i in range(2)]
            for blk in range(5):
                i0 = blk * 4
                nt = min(4, NT - i0)
                mb = MB[blk % 2]
                ld(mb, mb.ap[:, :, 0:nt * 128], mT_d[:, :, i0 * 128:(i0 + nt) * 128].rearrange('c p t -> p c t'),
                   r=[('mT', dc) for dc in range(8)])
                for ii in range(nt):
                    i = i0 + ii
                    pairs = []
                    for hf in range(2):
                        b = psum()
                        for dc in range(8):
                            mm(PS[b][:, 0:512], mb.ap[:, dc, ii * 128:(ii + 1) * 128], WO.ap[:, dc, hf * 512:(hf + 1) * 512], dc == 0, dc == 7,
                               mb.k2(dc, ii * 128, (ii + 1) * 128) + WO.k2(dc, hf * 512, (hf + 1) * 512), [PSK[b]])
                        pairs.append((b, hf))
                    post_tile(i, pairs, G, GB, LNG, LNB, TMPB)
                    ln_to_hT(i, 2)

        FB = [(0, CT)] + [(CT + 410 * i, min(CT + 410 * (i + 1), T)) for i in range(5)]

        def ffn1(l):
            UB = [[Buf(RA, 4096 * (2 * i + s), (1024,), F32) for s in range(2)] for i in range(2)]
            CV = [[Buf(RA, 16384 + 4096 * (2 * i + s), (1024,), F32) for s in range(2)] for i in range(2)]
            AT = [Buf(RA, 32768 + 4608 * i, (T,), BF16) for i in range(2)]
            pf = PFM.ap
            for fc in range(NFC):
                W = Buf(RW, 4096 * (fc % 8), (8, 256), BF16)
                ld(W, W.ap, di['ffn_w1x'][l, :, fc * 256:(fc + 1) * 256].rearrange('(kc p) n -> p kc n', p=128))
                at = AT[fc % 2]
                for bi, (t0, t1) in enumerate(FB):
                    n = t1 - t0
                    lo = t0 - 1 if bi not in (0, 1) else t0
                    hi = t1 + 1 if bi not in (0, 5) else t1
                    padl = 1 if lo == t0 else 0
                    nn = hi - lo
                    ub = UB[bi % 2]
                    cv = CV[bi % 2]
                    for s in range(2):
                        b = psum()
                        for kc in range(8):
                            mm(PS[b][:, 0:nn], W.ap[:, kc, s * 128:(s + 1) * 128], HT.ap[:, kc, lo:hi], kc == 0, kc == 7,
                               W.k2(kc, s * 128, (s + 1) * 128) + HT.k2(kc, lo, hi), [PSK[b]])
                        u = ub[s]
                        col = s * NFC + fc
                        bcol = PFM_OFF['fb1'] + col
                        if padl:
                            mset(u.ap[:, 0:1], 0.0, u.k())
                        if hi == t1:
                            mset(u.ap[:, padl + nn:padl + nn + 1], 0.0, u.k())
                        act(u.ap[:, padl:padl + nn], PS[b][:, 0:nn], AF.Identity, [PSK[b]] + PFM.k(), u.k(), bias=pf[:, bcol:bcol + 1], scale=1.0)
                        w0 = PFM_OFF['fcw'] + 0 * 44 + col
                        w1 = PFM_OFF['fcw'] + 1 * 44 + col
                        w2 = PFM_OFF['fcw'] + 2 * 44 + col
                        cbc = PFM_OFF['fcb'] + col
                        c_ = cv[s]
                        act(c_.ap[:, 0:n], u.ap[:, 0:n], AF.Identity, u.k() + PFM.k(), c_.k(), bias=pf[:, cbc:cbc + 1], scale=pf[:, w0:w0 + 1])
                        stt(c_.ap[:, 0:n], u.ap[:, 1:n + 1], pf[:, w1:w1 + 1], c_.ap[:, 0:n], ALU.mult, ALU.add, u.k() + PFM.k() + c_.k(), c_.k())
                        stt(c_.ap[:, 0:n], u.ap[:, 2:n + 2], pf[:, w2:w2 + 1], c_.ap[:, 0:n], ALU.mult, ALU.add, u.k() + PFM.k() + c_.k(), c_.k())
                    act(cv[1].ap[:, 0:n], cv[1].ap[:, 0:n], AF.Silu, cv[1].k(), cv[1].k())
                    tt(at.ap[:, t0:t1], cv[1].ap[:, 0:n], cv[0].ap[:, 0:n], ALU.mult, cv[0].k() + cv[1].k(), at.k(t0, t1))
                stq(at, aT_d[fc], at.ap, w=[('aT', fc)])

        def ffn2(l, last):
            G, GB, LNG, LNB = make_gate_tiles(l, 5, 'fb2', 'badag2', (RH, 0), 0)
            bcast_row(LNG, 'ln2g', 1024)
            bcast_row(LNB, 'ln2b', 1024)
            TMPB = [Buf(RH, 24576 + 2048 * i, (512,), F32) for i in range(2)]
            W2 = Buf(RA, 0, (NFC, D), BF16)
            ld(W2, W2.ap, di['ffn_w2'][l].rearrange('(fc p) n -> p fc n', p=128))
            AB = [Buf(RW, 11264 * i, (NFC, 256), BF16) for i in range(2)]
            toks = []
            for blk in range(NT // 2):
                ab = AB[blk % 2]
                ld(ab, ab.ap, aT_d[:, :, blk * 256:(blk + 1) * 256].rearrange('c p t -> p c t'),
                   r=[('aT', fc) for fc in range(NFC)])
                for ii in range(2):
                    i = blk * 2 + ii
                    pairs = []
                    for hf in range(2):
                        b = psum()
                        for fc in range(NFC):
                            mm(PS[b][:, 0:512], ab.ap[:, fc, ii * 128:(ii + 1) * 128], W2.ap[:, fc, hf * 512:(hf + 1) * 512], fc == 0, fc == NFC - 1,
                               ab.k2(fc, ii * 128, (ii + 1) * 128) + W2.k2(fc, hf * 512, (hf + 1) * 512), [PSK[b]])
                        pairs.append((b, hf))
                    post_tile(i, pairs, G, GB, LNG, LNB, TMPB)
                    if last and i >= 2:
                        toks.append(stq(XS, out_d[(i - 2) * 128:(i - 1) * 128, :], XS.ap[:, i, :], r=XS.k2(i), sub=1))
            return toks

        out_toks = []
        for l in range(nl):
            l_cur[0] = l
            lam_init = 0.8 - 0.6 * math.exp(-0.3 * l)
            ld(PFM, PFM.ap, di['pfm'][l])
            phase0_ada(l, [0, 1, 3, 4])
            for i in range(NT):
                ln_to_hT(i, 0)
            if stop == 'p1':
                break
            mixer_lru(l)
            if stop == 'lru':
                break
            mixer_diff(l, lam_init)
            if stop in ('diff', 'diff0', 'diff1', 'diff2'):
                break
            mixer_gqa(l)
            if stop == 'gqa':
                break
            mixer_na(l)
            if stop == 'na':
                break
            merge(l)
            if stop == 'merge':
                break
            wout_phase(l)
            if stop == 'wout':
                break
            ffn1(l)
            if stop == 'ffn1':
                break
            out_toks = ffn2(l, l == nl - 1)
        if debug:
            xd = dbg_d.rearrange('(i p) d -> p i d', p=128)
            for i in range(NT):
                out_toks.append(stq(XS, xd[:, i, :], XS.ap[:, i, :], r=XS.k2(i), sub=2))
            for kc in range(8 if stop is not None else 0):
                out_toks.append(stq(HT, dbgh_d[kc], HT.ap[:, kc, :], r=HT.k2(kc), sub=2))
        S.final_wait('sp', out_toks + [t for t in S.last_w.values() if t[2] == 'dma'])
        S.emit(st)
        build.stats = dict(n={e: len(S.q[e]) for e in S.ENGS}, nwaits=S.nwaits, nsems=S.nsems)
    return nc


_CACHE = {}


def kernel(**inputs):
    shared, per_core = prep_inputs(inputs)
    if 'nc' not in _CACHE:
        _CACHE['nc'] = build()
    nc = _CACHE['nc']
    in_maps = []
    for b in range(NCORES):
        m = dict(shared)
        m.update(per_core[b])
        in_maps.append(m)
    res = run_bass_kernel_spmd(nc, in_maps, core_ids=list(range(NCORES)))
    out = np.stack([np.asarray(res.results[b]['out'], np.float32) for b in range(NCORES)], 0)
    return out
```

```python
from contextlib import ExitStack
import math
import numpy as np
import concourse.bass as bass
import concourse.mybir as mybir
from concourse.bass_utils import run_bass_kernel_spmd

F32 = mybir.dt.float32
BF16 = mybir.dt.bfloat16
AF = mybir.ActivationFunctionType
ALU = mybir.AluOpType
AX = mybir.AxisListType

D = 1024
SL = 2048
CT = 256
T = SL + CT
NT = T // 128
L = 4
DFF = 2816
NFC = DFF // 128
ALPHA = (2 * L) ** 0.25
EPS = 1e-6
TQ = [(0, 512), (512, 1024), (1024, 1536), (1536, 2048), (2048, 2304)]
NCORES = 8
NFM = 28
NTM = 640
GOFF = NFM * 128 + NTM
NCOLX = GOFF + 4096
EPOCH = 16000


class Sched:
    ENGS = ['pe', 'act', 'dve', 'pool', 'sp']

    def __init__(self, nc):
        self.nc = nc
        self.q = {e: [] for e in self.ENGS}
        self.n = {e: 0 for e in self.ENGS}
        self.last_w = {}
        self.readers = {}
        self.seen = {e: {} for e in self.ENGS}
        self.dma_cnt = {}
        self.nwaits = 0

    def _deps(self, eng, reads, writes):
        waits = {}
        seen = self.seen[eng]
        is_pe = eng == 'pe'

        def add(t):
            sk, val, teng = t
            if is_pe and teng == 'pe':
                return
            if seen.get(sk, 0) >= val:
                return
            if waits.get(sk, 0) < val:
                waits[sk] = val
        for r in reads:
            t = self.last_w.get(r)
            if t is not None:
                add(t)
        for w in writes:
            t = self.last_w.get(w)
            if t is not None:
                add(t)
            rd = self.readers.get(w)
            if rd:
                for sk, (val, teng) in rd.items():
                    add((sk, val, teng))
        for sk, val in waits.items():
            seen[sk] = val
        return list(waits.items())

    def _commit(self, tok, reads, writes):
        sk, val, teng = tok
        for r in reads:
            d = self.readers.get(r)
            if d is None:
                d = {}
                self.readers[r] = d
            d[sk] = (val, teng)
        for w in writes:
            self.last_w[w] = tok
            self.readers[w] = None

    def op(self, eng, fn, reads=(), writes=()):
        waits = self._deps(eng, reads, writes)
        n = self.n[eng]
        self.n[eng] = n + 1
        sk = 'E_%s_%d' % (eng, n // EPOCH)
        tok = (sk, n % EPOCH + 1, eng)
        self.q[eng].append((fn, waits, (sk, 1)))
        self.nwaits += len(waits)
        self._commit(tok, reads, writes)
        return tok

    def dma(self, eng, fn, key, reads=(), writes=()):
        waits = self._deps(eng, reads, writes)
        c = self.dma_cnt.get(key, 0)
        self.dma_cnt[key] = c + 1
        sk = 'D_%s_%d' % (key, c // 1000)
        tok = (sk, (c % 1000 + 1) * 16, 'dma')
        self.q[eng].append((fn, waits, (sk, 16)))
        self.nwaits += len(waits)
        self._commit(tok, reads, writes)
        return tok

    def final_wait(self, eng, toks):
        waits = {}
        for (sk, val, _) in toks:
            if waits.get(sk, 0) < val:
                waits[sk] = val
        self.q[eng].append((None, list(waits.items()), None))

    def emit(self, stack):
        nc = self.nc
        names = set()
        for e in self.ENGS:
            for (fn, waits, inc) in self.q[e]:
                for sk, _ in waits:
                    names.add(sk)
                if inc is not None:
                    names.add(inc[0])
        sems = {}
        for i, sk in enumerate(sorted(names)):
            sems[sk] = stack.enter_context(nc.semaphore('s%d' % i))
        self.nsems = len(sems)
        block = stack.enter_context(nc.Block())
        handles = {'pe': block.tensor, 'act': block.scalar, 'dve': block.vector,
                   'pool': block.gpsimd, 'sp': block.sync}
        for e in self.ENGS:
            ops = self.q[e]

            def body(eng, ops=ops):
                for (fn, waits, inc) in ops:
                    for sk, val in waits:
                        eng.wait_ge(sems[sk], val)
                    if fn is not None:
                        inst = fn(eng)
                        inst.then_inc(sems[inc[0]], inc[1])
            handles[e](body)


GRAN = 512
import os
KATTN = int(os.environ.get('KATTN', '9'))


class Region:
    def __init__(self, nc, st, name, nbytes, gran=GRAN):
        self.name = name
        self.nbytes = nbytes
        self.gran = gran
        self.t = st.enter_context(nc.sbuf_tensor(name, [128, nbytes // 4], F32))


class Buf:
    def __init__(self, reg, off, shape, dtype):
        self.reg = reg
        self.off = off
        self.shape = tuple(shape)
        self.esz = 4 if dtype == F32 else 2
        n = 1
        for s in shape:
            n *= s
        self.n = n
        nb = n * self.esz
        assert off % 4 == 0 and nb % 4 == 0 and off + nb <= reg.nbytes, (reg.name, off, nb, reg.nbytes)
        ap = reg.t[:, off // 4:(off + nb) // 4]
        if dtype != F32:
            ap = ap.bitcast(dtype)
        if len(shape) == 2:
            ap = ap.rearrange('p (a b) -> p a b', b=shape[1])
        elif len(shape) == 3:
            ap = ap.rearrange('p (a b c) -> p a b c', b=shape[1], c=shape[2])
        elif len(shape) == 4:
            ap = ap.rearrange('p (a b c d) -> p a b c d', b=shape[1], c=shape[2], d=shape[3])
        self.ap = ap
        self.end = off + nb

    def k(self, lo=0, hi=None):
        if hi is None:
            hi = self.n
        b0 = (self.off + lo * self.esz) // self.reg.gran
        b1 = (self.off + hi * self.esz - 1) // self.reg.gran
        nm = self.reg.name
        return [(nm, g) for g in range(b0, b1 + 1)]

    def k2(self, i, lo=0, hi=None):
        inner = self.n // self.shape[0]
        if hi is None:
            hi = inner
        return self.k(i * inner + lo, i * inner + hi)


OFF = dict(a_x=0, a_g=256, d_q=512, d_k=768, d_v=1024, g_q=1280, g_k=1536, g_v=1664,
           n_q=1792, n_k=2048, n_v=2304, gates=2560)


def _perm(base, n, grp):
    idx = np.arange(n)
    g = idx // grp
    j = idx % grp
    return base + g * grp + (j + grp // 2) % grp


def _col_index():
    ar = np.arange
    kd0 = np.concatenate([ar(64), ar(64)])
    kd1 = kd0 + 64
    p64 = (ar(64) + 32) % 64
    kp0 = np.concatenate([p64, p64])
    kp1 = kp0 + 64
    g3 = np.concatenate([np.concatenate([min(3 * c + s_, 7) * 32 + ar(32) for s_ in (0, 1, 2, 0)]) for c in range(3)])
    fm = np.concatenate([
        OFF['a_x'] + ar(256), OFF['a_g'] + ar(256),
        OFF['d_q'] + g3, _perm(OFF['d_q'], 256, 32)[g3],
        OFF['d_k'] + g3, _perm(OFF['d_k'], 256, 32)[g3],
        OFF['g_q'] + ar(256), _perm(OFF['g_q'], 256, 64),
        OFF['g_k'] + kd0, OFF['g_k'] + kd1, OFF['g_k'] + kp0, OFF['g_k'] + kp1,
        OFF['n_q'] + ar(256), OFF['n_k'] + ar(256)])
    assert fm.size == NFM * 128
    tm = np.concatenate([OFF['d_v'] + ar(256), OFF['g_v'] + ar(128), OFF['n_v'] + ar(256)])
    gate = np.concatenate([OFF['gates'] + j * 1024 + dc * 128 + ar(128) for dc in range(8) for j in range(4)])
    return fm, tm, gate


PFM_LAYOUT = [('bada', 48), ('binfm', NFM), ('bgate', 32), ('lcw', 8), ('lcb', 2), ('lba', 4), ('lbx', 4),
              ('llam', 4), ('gq', 1), ('gqp', 1), ('gk', 1), ('gkp', 1), ('fb1', 44), ('fcw', 132), ('fcb', 44)]
PFM_OFF = {}
_o = 0
for _n, _c in PFM_LAYOUT:
    PFM_OFF[_n] = _o
    _o += _c
NPF = _o
PROW_LAYOUT = [('binv', NTM), ('badag1', 1024), ('badag2', 1024), ('bout', 1024), ('fb2', 1024), ('ln1g', 1024),
               ('ln1b', 1024), ('ln2g', 1024), ('ln2b', 1024), ('subln', 64), ('dlam', 128)]
PROW_OFF = {}
_o = 0
for _n, _c in PROW_LAYOUT:
    PROW_OFF[_n] = _o
    _o += _c
NROW = _o


def _fm(v):
    v = np.asarray(v, np.float32).reshape(-1, 128)
    return np.ascontiguousarray(v.T)


def _rope_tables():
    t = np.arange(SL)
    row = (t // 64).astype(np.float32)
    col = (t % 64).astype(np.float32)
    out = np.zeros((4, 128, T), np.float32)
    for ti, dim in ((0, 32), (2, 64)):
        nf = dim // 4
        inv = (np.float32(10000.0) ** (-np.arange(nf, dtype=np.float32) / np.float32(nf))).astype(np.float32)
        ang = np.concatenate([row[:, None] * inv, col[:, None] * inv], -1).astype(np.float32)
        cos = np.cos(ang).astype(np.float32)
        sin = np.sin(ang).astype(np.float32)
        half = dim // 2
        for p in range(128):
            j = p % dim
            out[ti, p, :CT] = 1.0
            out[ti + 1, p, :CT] = 0.0
            out[ti, p, CT:] = cos[:, j % half]
            out[ti + 1, p, CT:] = -sin[:, j % half] if j < half else sin[:, j % half]
    return out


NA_HEAD_ORDER = [0, 2, 1, 3]
NA_VARIANT_R = [0, 2, 4, 28, 30]


def _na_sr(r):
    return min(max(r - 4, 0), 22)


def _na_variant(r):
    if r == 0:
        return 0
    if r == 2:
        return 1
    if r == 28:
        return 3
    if r == 30:
        return 4
    return 2


def _na_bias_tables(rpb):
    Ln = rpb.shape[0]
    out = np.full((Ln, 5, 128, 5, 4, 128), -30000.0, np.float32)
    q = np.arange(128)
    for v, r in enumerate(NA_VARIANT_R):
        sr = _na_sr(r)
        qr = r + q // 64
        qc = q % 64
        r0 = np.clip(qr - 4, 0, 24)
        c0 = np.clip(qc - 8, 0, 48)
        for kt in range(5):
            kk = np.arange(128)
            kr = sr + 2 * kt + kk // 64
            kc = kk % 64
            inwin = ((kr[:, None] >= r0[None, :]) & (kr[:, None] < r0[None, :] + 8) &
                     (kc[:, None] >= c0[None, :]) & (kc[:, None] < c0[None, :] + 16))
            ro = np.clip(kr[:, None] - qr[None, :] + 7, 0, 14)
            co = np.clip(kc[:, None] - qc[None, :] + 15, 0, 30)
            for slot, h in enumerate(NA_HEAD_ORDER):
                vals = rpb[:, h][:, ro, co]
                cur = out[:, v, :, kt, slot, :]
                out[:, v, :, kt, slot, :] = np.where(inwin[None], vals, cur)
    return out.reshape(Ln, 5, 128, 5 * 4 * 128)


def prep_inputs(inp):
    f32 = lambda a: np.ascontiguousarray(np.asarray(a, np.float32))
    fm, tm, gate = _col_index()
    colx = np.concatenate([fm, tm, gate])
    w_in = f32(inp['w_in'])
    b_in = f32(inp['b_in'])
    shared = {}
    shared['w_ada'] = f32(inp['w_ada'])
    shared['w_inx'] = np.ascontiguousarray(w_in[:, :, colx])
    wb = f32(inp['w_branch'])
    wb = wb.reshape(L, 4, 2, 128, 8, 128)
    shared['w_brx'] = np.ascontiguousarray(wb.transpose(0, 4, 3, 1, 2, 5)).reshape(L, 8, 128, 1024)
    shared['w_out'] = f32(inp['w_out'])
    w1 = f32(inp['ffn_w1']).reshape(L, D, 2, NFC, 128)
    shared['ffn_w1x'] = np.ascontiguousarray(w1.transpose(0, 1, 3, 2, 4)).reshape(L, D, 2 * DFF)
    shared['ffn_w2'] = f32(inp['ffn_w2'])
    wa = f32(inp['lru_w_a'])
    wx = f32(inp['lru_w_x'])
    wbd = np.zeros((L, 2, 2, 2, 128, 128), np.float32)
    for gi, wsrc in enumerate((wa, wx)):
        for d in range(2):
            for c in range(2):
                wbd[:, gi, d, c, 0:64, 0:64] = wsrc[:, d, 2 * c]
                wbd[:, gi, d, c, 64:128, 64:128] = wsrc[:, d, 2 * c + 1]
    shared['lru_wbd'] = wbd.reshape(L, 8, 128, 128)
    shared['na_bias'] = _na_bias_tables(f32(inp['na_rpb']))
    shared['rope'] = _rope_tables()
    shared['ident'] = np.eye(128, dtype=np.float32)
    pfm = np.zeros((L, 128, NPF), np.float32)
    prow = np.zeros((L, NROW), np.float32)
    gqn = f32(inp['gqa_q_norm'])
    gkn = f32(inp['gqa_k_norm'])
    p64 = (np.arange(64) + 32) % 64
    b1 = f32(inp['ffn_b1'])
    cw = f32(inp['ffn_conv_w'])
    cb = f32(inp['ffn_conv_b'])
    for l in range(L):
        def put(name, arr):
            o = PFM_OFF[name]
            pfm[l, :, o:o + arr.shape[1]] = arr
        put('bada', _fm(inp['b_ada'][l]))
        put('binfm', _fm(b_in[l][fm]))
        put('bgate', _fm(b_in[l][gate]))
        put('lcw', _fm(f32(inp['lru_conv_w'])[l].reshape(-1)))
        put('lcb', _fm(inp['lru_conv_b'][l]))
        put('lba', _fm(f32(inp['lru_b_a'])[l].reshape(-1)))
        put('lbx', _fm(f32(inp['lru_b_x'])[l].reshape(-1)))
        put('llam', _fm(f32(inp['lru_lambda'])[l].reshape(-1)))
        put('gq', _fm(np.tile(gqn[l], 2)))
        put('gqp', _fm(np.tile(gqn[l][p64], 2)))
        put('gk', _fm(np.tile(gkn[l], 2)))
        put('gkp', _fm(np.tile(gkn[l][p64], 2)))
        put('fb1', _fm(b1[l]))
        put('fcw', _fm(cw[l].reshape(-1)))
        put('fcb', _fm(cb[l]))

        def putr(name, arr):
            o = PROW_OFF[name]
            prow[l, o:o + arr.size] = np.asarray(arr, np.float32).reshape(-1)
        putr('binv', b_in[l][tm])
        putr('badag1', inp['b_ada'][l][2 * D:3 * D])
        putr('badag2', inp['b_ada'][l][5 * D:6 * D])
        putr('bout', inp['b_out'][l])
        putr('fb2', inp['ffn_b2'][l])
        putr('ln1g', inp['ln1_g'][l])
        putr('ln1b', inp['ln1_b'][l])
        putr('ln2g', inp['ln2_g'][l])
        putr('ln2b', inp['ln2_b'][l])
        putr('subln', inp['diff_subln'][l])
        putr('dlam', f32(inp['diff_lambda'])[l].reshape(-1))
    shared['pfm'] = pfm
    shared['prow'] = prow
    x = f32(inp['x'])
    ctx = f32(inp['ctx'])
    c = f32(inp['c'])
    cc = f32(inp['c_ctx'])
    per_core = []
    for b in range(x.shape[0]):
        cv = np.stack([c[b], cc], 0)
        cvT = np.ascontiguousarray(cv.reshape(2, 8, 128).transpose(2, 1, 0))
        per_core.append({'xin': np.concatenate([ctx[b], x[b]], 0), 'cvT': cvT})
    return shared, per_core


SHARED_SHAPES = dict(w_ada=[L, D, 6 * D], w_inx=[L, D, NCOLX], w_brx=[L, 8, 128, 1024], w_out=[L, D, D],
                     ffn_w1x=[L, D, 2 * DFF], ffn_w2=[L, DFF, D], lru_wbd=[L, 8, 128, 128],
                     na_bias=[L, 5, 128, 2560], rope=[4, 128, T], ident=[128, 128], pfm=[L, 128, NPF],
                     prow=[L, NROW], xin=[T, D], cvT=[128, 8, 2])


def build(nl=L, debug=False, stop=None):
    nc = bass.Bass('TRN2', target_bir_lowering=False)
    di = {n: nc.dram_tensor(n, list(s), F32, kind='ExternalInput').ap() for n, s in SHARED_SHAPES.items()}
    out_d = nc.dram_tensor('out', [SL, D], F32, kind='ExternalOutput').ap()
    skind = 'ExternalOutput' if debug else 'Internal'
    yT_d = nc.dram_tensor('yT_d', [8, 128, T], BF16, kind=skind).ap()
    mT_d = nc.dram_tensor('mT_d', [8, 128, T], BF16, kind=skind).ap()
    aT_d = nc.dram_tensor('aT_d', [NFC, 128, T], BF16, kind=skind).ap()
    dbg_d = None
    if debug:
        dbg_d = nc.dram_tensor('dbg', [T, D], F32, kind='ExternalOutput').ap()
        dbgh_d = nc.dram_tensor('dbgh', [8, 128, T], BF16, kind='ExternalOutput').ap()

    with ExitStack() as st:
        RX = Region(nc, st, 'RX', 72 * 1024)
        RH = Region(nc, st, 'RH', 36 * 1024)
        RW = Region(nc, st, 'RW', 32 * 1024)
        RA = Region(nc, st, 'RA', 54 * 1024)
        RC = Region(nc, st, 'RC', 13824, gran=64)
        PS = [st.enter_context(nc.psum_tensor('ps%d' % i, [128, 512], F32)) for i in range(8)]
        PSK = [('ps', i) for i in range(8)]
        S = Sched(nc)

        def mm(out, lhsT, rhs, start, stop_, r, w, skip=False):
            S.op('pe', lambda e: e.matmul(out, lhsT=lhsT, rhs=rhs, start=start, stop=stop_, skip_group_check=skip), r, w)

        def tr(out, in_, ident, r, w):
            S.op('pe', lambda e: e.transpose(out, in_, ident), r, w)

        def act(out, in_, func, r, w, bias=None, scale=None):
            kw = {}
            if bias is not None:
                kw['bias'] = bias
            if scale is not None:
                kw['scale'] = scale
            S.op('act', lambda e: e.activation(out=out, in_=in_, func=func, **kw), r, w)

        def ts(out, in0, s1, s2, op0, op1, r, w, eng='dve'):
            if op1 is None:
                S.op(eng, lambda e: e.tensor_scalar(out=out, in0=in0, scalar1=s1, scalar2=None, op0=op0), r, w)
            else:
                S.op(eng, lambda e: e.tensor_scalar(out=out, in0=in0, scalar1=s1, scalar2=s2, op0=op0, op1=op1), r, w)

        def tt(out, in0, in1, op, r, w, eng='dve'):
            S.op(eng, lambda e: e.tensor_tensor(out=out, in0=in0, in1=in1, op=op), r, w)

        def stt(out, in0, scalar, in1, op0, op1, r, w):
            S.op('dve', lambda e: e.scalar_tensor_tensor(out=out, in0=in0, scalar=scalar, in1=in1, op0=op0, op1=op1), r, w)

        def cp(out, in_, r, w, eng='dve'):
            S.op(eng, lambda e: e.tensor_copy(out=out, in_=in_), r, w)

        def mset(ap, val, w, eng='dve'):
            S.op(eng, lambda e: e.memset(ap, val), (), w)

        def ld(buf, out, in_, w=None, r=(), sub=0):
            key = 'L%s_%d_%d' % (buf.reg.name, buf.off, sub)
            if w is None:
                w = buf.k()
            if in_.dtype != out.dtype:
                return S.dma('pool', lambda e: e.dma_start(out=out, in_=in_, max_dma_last_dim=4096), key, r, w)
            return S.dma('pool', lambda e: e.dma_start(out=out, in_=in_), key, r, w)

        def stq(buf, out, in_, r=None, w=(), sub=0):
            key = 'S%s_%d_%d' % (buf.reg.name, buf.off, sub)
            if r is None:
                r = buf.k()
            return S.dma('sp', lambda e: e.dma_start(out=out, in_=in_), key, r, w)

        XS = Buf(RX, 0, (NT, D), F32)
        HT = Buf(RH, 0, (8, T), BF16)
        o = 0

        def rc(shape, dtype):
            nonlocal o
            b = Buf(RC, o, shape, dtype)
            o = (b.end + 63) // 64 * 64
            return b
        IDB = rc((128,), BF16)
        BONES = rc((128,), F32)
        PFM = rc((NPF,), F32)
        SILT = rc((8, 2), F32)
        SILB = rc((8, 2), BF16)
        SCB = rc((8, 2, 128), BF16)
        ONES = rc((128,), F32)
        MODF = rc((4, 8, 2), F32)
        STATS = rc((NT, 12), F32)
        MV = rc((NT, 2), F32)
        SD = rc((NT,), F32)
        RSTD = rc((NT,), F32)
        XN = [rc((D,), BF16) for _ in range(2)]
        LAMB = rc((8,), F32)
        LRUS = rc((16,), F32)
        SMALL = rc((64,), F32)
        EPSB = rc((2,), F32)

        ld(IDB, IDB.ap, di['ident'])
        mset(BONES.ap, 0.0, BONES.k())
        mset(BONES.ap[0:64, 0:64], 1.0, BONES.k())
        mset(BONES.ap[64:128, 64:128], 1.0, BONES.k())
        mset(ONES.ap, 1.0, ONES.k())
        mset(EPSB.ap[:, 0:1], EPS, EPSB.k())
        ld(SILT, SILT.ap, di['cvT'])
        act(SILT.ap, SILT.ap, AF.Silu, SILT.k(), SILT.k())
        cp(SILB.ap, SILT.ap, SILT.k(), SILB.k())
        for kc in range(8):
            for j in range(2):
                act(SCB.ap[:, kc, j, :], ONES.ap, AF.Identity, ONES.k() + SILT.k(), SCB.k(), scale=SILT.ap[:, kc, j:j + 1])
        xin_t = di['xin'].rearrange('(i p) d -> p i d', p=128)
        S.dma('sp', lambda e: e.dma_start(out=XS.ap, in_=xin_t), 'xin', (), XS.k())

        psr = [0]

        def psum(n=6):
            b = psr[0] % n
            psr[0] += 1
            return b

        def phase0_ada(l, groups):
            for g in groups:
                gi = {0: 0, 1: 1, 3: 2, 4: 3}[g]
                W = Buf(RW, 16384 * (gi % 2), (8, 1024), BF16)
                ld(W, W.ap, di['w_ada'][l, :, g * 1024:(g + 1) * 1024].rearrange('(kc p) n -> p kc n', p=128))
                for fc in range(8):
                    b = psum()
                    for kc in range(8):
                        mm(PS[b][:, 0:2], W.ap[:, kc, fc * 128:(fc + 1) * 128], SILB.ap[:, kc, :], kc == 0, kc == 7,
                           W.k2(kc, fc * 128, (fc + 1) * 128) + SILB.k(), [PSK[b]])
                    c0 = PFM_OFF['bada'] + g * 8 + fc
                    ts(MODF.ap[:, gi, fc, :], PS[b][:, 0:2], PFM.ap[:, c0:c0 + 1], 1.0 if g in (1, 4) else 0.0,
                       ALU.add, ALU.add, [PSK[b]] + PFM.k(), MODF.k())

        def ln_stats(i):
            for hf in range(2):
                S.op('dve', lambda e, hf=hf: e.bn_stats(out=STATS.ap[:, i, hf * 6:(hf + 1) * 6], in_=XS.ap[:, i, hf * 512:(hf + 1) * 512]),
                     XS.k2(i), STATS.k())
            S.op('dve', lambda e: e.bn_aggr(out=MV.ap[:, i, :], in_=STATS.ap[:, i, :]), STATS.k(), MV.k())
            act(SD.ap[:, i:i + 1], MV.ap[:, i, 1:2], AF.Sqrt, MV.k() + EPSB.k(), SD.k(), bias=EPSB.ap[:, 0:1], scale=1.0)
            S.op('dve', lambda e: e.reciprocal(out=RSTD.ap[:, i:i + 1], in_=SD.ap[:, i:i + 1]), SD.k(), RSTD.k())

        def ln_to_hT(i, gi):
            j = 1 if i < 2 else 0
            ln_stats(i)
            xn = XN[i % 2]
            ts(xn.ap, XS.ap[:, i, :], MV.ap[:, i, 0:1], RSTD.ap[:, i:i + 1], ALU.subtract, ALU.mult,
               XS.k2(i) + MV.k() + RSTD.k(), xn.k())
            b = 6 + (i % 2)
            pT = PS[b][:].bitcast(BF16).rearrange('p (a b) -> p a b', b=128)
            for kc in range(8):
                tr(pT[:, kc, :], xn.ap[:, kc * 128:(kc + 1) * 128], IDB.ap, xn.k() + IDB.k(), [PSK[b]])
            for kc in range(8):
                act(HT.ap[:, kc, i * 128:(i + 1) * 128], pT[:, kc, :], AF.Identity, [PSK[b]] + MODF.k(),
                    HT.k2(kc, i * 128, (i + 1) * 128), bias=MODF.ap[:, gi, kc, j:j + 1], scale=MODF.ap[:, gi + 1, kc, j:j + 1])

        def load_win(c0, ncols, off):
            W = Buf(RW, off, (8, ncols), BF16)
            ld(W, W.ap, di['w_inx'][l_cur[0], :, c0:c0 + ncols].rearrange('(kc p) n -> p kc n', p=128))
            return W

        def proj_fm(W, wc, tq, b):
            t0, t1 = TQ[tq]
            n = t1 - t0
            for kc in range(8):
                mm(PS[b][:, 0:n], W.ap[:, kc, wc * 128:(wc + 1) * 128], HT.ap[:, kc, t0:t1], kc == 0, kc == 7,
                   W.k2(kc, wc * 128, (wc + 1) * 128) + HT.k2(kc, t0, t1), [PSK[b]])
            return n

        def bcast_row(buf, name, n, key=None):
            o_ = PROW_OFF[name]
            ld(buf, buf.ap, di['prow'][l_cur[0], o_:o_ + n].partition_broadcast(128))

        l_cur = [0]

        def mixer_lru(l):
            XP = Buf(RA, 0, (2310,), F32)
            XC = Buf(RA, 9728, (T,), F32)
            XCB = Buf(RA, 19456, (T,), BF16)
            AA = Buf(RA, 24576, (T,), F32)
            BB = [Buf(RA, 34304, (T,), F32), Buf(RA, 44032, (T,), F32)]
            TM = [Buf(RW, 8192 + 2048 * i, (512,), F32) for i in range(4)]
            WBD = Buf(RW, 16384, (8, 128), BF16)
            YST = [Buf(RW, 18432 + 1024 * i, (512,), BF16) for i in range(2)]
            ld(WBD, WBD.ap, di['lru_wbd'][l].rearrange('g p n -> p g n'))
            pf = PFM.ap
            o_l = PFM_OFF['llam']
            act(LRUS.ap[:, 0:4], pf[:, o_l:o_l + 4], AF.Exp, PFM.k(), LRUS.k(), scale=-1.0)
            act(LRUS.ap[:, 0:4], LRUS.ap[:, 0:4], AF.Ln, LRUS.k(), LRUS.k(), bias=1.0, scale=1.0)
            ts(LRUS.ap[:, 4:8], LRUS.ap[:, 0:4], -8.0, None, ALU.mult, None, LRUS.k(), LRUS.k())
            ts(LRUS.ap[:, 8:12], LRUS.ap[:, 0:4], -16.0, None, ALU.mult, None, LRUS.k(), LRUS.k())
            Wx = load_win(0, 512, 0)
            for c in range(2):
                mset(XP.ap, 0.0, XP.k())
                bcol = PFM_OFF['binfm'] + c
                for tq in range(5):
                    b = psum()
                    n = proj_fm(Wx, c, tq, b)
                    t0, t1 = TQ[tq]
                    if tq == 0:
                        act(XP.ap[:, 2:258], PS[b][:, 0:256], AF.Identity, [PSK[b]] + PFM.k(), XP.k(), bias=pf[:, bcol:bcol + 1], scale=1.0)
                        act(XP.ap[:, 261:517], PS[b][:, 256:512], AF.Identity, [PSK[b]] + PFM.k(), XP.k(), bias=pf[:, bcol:bcol + 1], scale=1.0)
                    else:
                        act(XP.ap[:, t0 + 5:t1 + 5], PS[b][:, 0:n], AF.Identity, [PSK[b]] + PFM.k(), XP.k(), bias=pf[:, bcol:bcol + 1], scale=1.0)
                ow = PFM_OFF['lcw']
                ob = PFM_OFF['lcb'] + c
                for (x0, t0, n) in ((2, 0, 256), (261, 256, 2048)):
                    act(XC.ap[:, t0:t0 + n], XP.ap[:, x0 - 2:x0 - 2 + n], AF.Identity, XP.k() + PFM.k(), XC.k(t0, t0 + n),
                        bias=pf[:, ob:ob + 1], scale=pf[:, ow + c:ow + c + 1])
                    for k in range(1, 4):
                        stt(XC.ap[:, t0:t0 + n], XP.ap[:, x0 - 2 + k:x0 - 2 + k + n], pf[:, ow + 2 * k + c:ow + 2 * k + c + 1],
                            XC.ap[:, t0:t0 + n], ALU.mult, ALU.add, XP.k() + PFM.k() + XC.k(t0, t0 + n), XC.k(t0, t0 + n))
                cp(XCB.ap, XC.ap, XC.k(), XCB.k())
                for d in range(2):
                    ia = PFM_OFF['lba'] + d * 2 + c
                    ix = PFM_OFF['lbx'] + d * 2 + c
                    sc8 = LRUS.ap[:, 4 + d * 2 + c:5 + d * 2 + c]
                    sc16 = LRUS.ap[:, 8 + d * 2 + c:9 + d * 2 + c]
                    for tq in range(5):
                        t0, t1 = TQ[tq]
                        n = t1 - t0
                        b1 = psum()
                        mm(PS[b1][:, 0:n], WBD.ap[:, 0 * 4 + d * 2 + c, :], XCB.ap[:, t0:t1], True, True, WBD.k() + XCB.k(t0, t1), [PSK[b1]])
                        b2 = psum()
                        mm(PS[b2][:, 0:n], WBD.ap[:, 1 * 4 + d * 2 + c, :], XCB.ap[:, t0:t1], True, True, WBD.k() + XCB.k(t0, t1), [PSK[b2]])
                        r_, a2_, i_ = TM[0], TM[1], TM[2]
                        act(r_.ap[:, 0:n], PS[b1][:, 0:n], AF.Sigmoid, [PSK[b1]] + PFM.k(), r_.k(), bias=pf[:, ia:ia + 1], scale=1.0)
                        act(i_.ap[:, 0:n], PS[b2][:, 0:n], AF.Sigmoid, [PSK[b2]] + PFM.k(), i_.k(), bias=pf[:, ix:ix + 1], scale=1.0)
                        act(AA.ap[:, t0:t1], r_.ap[:, 0:n], AF.Exp, r_.k() + LRUS.k(), AA.k(t0, t1), scale=sc8)
                        act(a2_.ap[:, 0:n], r_.ap[:, 0:n], AF.Exp, r_.k() + LRUS.k(), a2_.k(), scale=sc16)
                        act(a2_.ap[:, 0:n], a2_.ap[:, 0:n], AF.Ln, a2_.k(), a2_.k(), bias=1.0, scale=-1.0)
                        act(a2_.ap[:, 0:n], a2_.ap[:, 0:n], AF.Exp, a2_.k(), a2_.k(), scale=0.5)
                        tt(i_.ap[:, 0:n], i_.ap[:, 0:n], a2_.ap[:, 0:n], ALU.mult, i_.k() + a2_.k(), i_.k())
                        tt(BB[d].ap[:, t0:t1], i_.ap[:, 0:n], XC.ap[:, t0:t1], ALU.mult, i_.k() + XC.k(t0, t1), BB[d].k(t0, t1))
                    Bd = BB[d]
                    if d == 0:
                        S.op('dve', lambda e, Bd=Bd: e.tensor_tensor_scan(out=Bd.ap[:, 0:CT], data0=AA.ap[:, 0:CT], data1=Bd.ap[:, 0:CT],
                                                                          initial=0.0, op0=ALU.mult, op1=ALU.add),
                             AA.k(0, CT) + Bd.k(0, CT), Bd.k(0, CT))
                        S.op('dve', lambda e, Bd=Bd: e.tensor_tensor_scan(out=Bd.ap[:, CT:T], data0=AA.ap[:, CT:T], data1=Bd.ap[:, CT:T],
                                                                          initial=Bd.ap[:, CT - 1:CT], op0=ALU.mult, op1=ALU.add),
                             AA.k() + Bd.k(), Bd.k(CT, T))
                    else:
                        S.op('dve', lambda e, Bd=Bd: e.tensor_tensor_scan(out=Bd.ap[:, 0:CT][:, ::-1], data0=AA.ap[:, 0:CT][:, ::-1],
                                                                          data1=Bd.ap[:, 0:CT][:, ::-1], initial=0.0, op0=ALU.mult, op1=ALU.add),
                             AA.k(0, CT) + Bd.k(0, CT), Bd.k(0, CT))
                        S.op('dve', lambda e, Bd=Bd: e.tensor_tensor_scan(out=Bd.ap[:, CT:T][:, ::-1], data0=AA.ap[:, CT:T][:, ::-1],
                                                                          data1=Bd.ap[:, CT:T][:, ::-1], initial=Bd.ap[:, 0:1],
                                                                          op0=ALU.mult, op1=ALU.add),
                             AA.k() + Bd.k(), Bd.k(CT, T))
                gcol = PFM_OFF['binfm'] + 2 + c
                for tq in range(5):
                    t0, t1 = TQ[tq]
                    b = psum()
                    n = proj_fm(Wx, 2 + c, tq, b)
                    g_, u_, s_ = TM[0], TM[1], TM[2]
                    act(g_.ap[:, 0:n], PS[b][:, 0:n], AF.Identity, [PSK[b]] + PFM.k(), g_.k(), bias=pf[:, gcol:gcol + 1], scale=1.0)
                    tt(u_.ap[:, 0:n], g_.ap[:, 0:n], g_.ap[:, 0:n], ALU.mult, g_.k(), u_.k())
                    ts(u_.ap[:, 0:n], u_.ap[:, 0:n], 0.044715, 1.0, ALU.mult, ALU.add, u_.k(), u_.k())
                    tt(u_.ap[:, 0:n], u_.ap[:, 0:n], g_.ap[:, 0:n], ALU.mult, u_.k() + g_.k(), u_.k())
                    act(s_.ap[:, 0:n], u_.ap[:, 0:n], AF.Sigmoid, u_.k(), s_.k(), scale=2.0 * math.sqrt(2.0 / math.pi))
                    tt(s_.ap[:, 0:n], s_.ap[:, 0:n], g_.ap[:, 0:n], ALU.mult, s_.k() + g_.k(), s_.k())
                    tt(u_.ap[:, 0:n], BB[0].ap[:, t0:t1], BB[1].ap[:, t0:t1], ALU.add, BB[0].k(t0, t1) + BB[1].k(t0, t1), u_.k())
                    ys = YST[tq % 2]
                    tt(ys.ap[:, 0:n], u_.ap[:, 0:n], s_.ap[:, 0:n], ALU.mult, u_.k() + s_.k(), ys.k())
                    stq(ys, yT_d[c, :, t0:t1], ys.ap[:, 0:n], w=[('yT', c, tq)])

        def v_proj(Wv, wc0, nh, VA, bvt, bo):
            ncol = nh * 64
            mset(VA.ap[:, :, :, 64:65], 1.0, VA.k())
            for i in range(NT):
                b = psum()
                for kc in range(8):
                    mm(PS[b][:, 0:ncol], HT.ap[:, kc, i * 128:(i + 1) * 128], Wv.ap[:, kc, wc0:wc0 + ncol], kc == 0, kc == 7,
                       HT.k2(kc, i * 128, (i + 1) * 128) + Wv.k2(kc, wc0, wc0 + ncol), [PSK[b]])
                tt(VA.ap[:, i, :, 0:64], PS[b][:, 0:ncol].rearrange('p (h d) -> p h d', d=64),
                   bvt.ap[:, bo:bo + ncol].rearrange('p (h d) -> p h d', d=64), ALU.add, [PSK[b]] + bvt.k(), VA.k2(i))

        def y_out(ybuf, branch, q0, nsub, YT2):
            for kc in range(2):
                b = 6 + (psr[0] % 2)
                psr[0] += 1
                pT = PS[b][:].bitcast(BF16)
                for j in range(nsub):
                    tr(pT[:, j * 128:(j + 1) * 128], ybuf.ap[:, j, kc * 128:(kc + 1) * 128], IDB.ap, ybuf.k() + IDB.k(), [PSK[b]])
                yt = YT2[kc]
                act(yt.ap[:, 0:nsub * 128], pT[:, 0:nsub * 128], AF.Copy, [PSK[b]], yt.k())
                stq(yt, yT_d[branch * 2 + kc, :, q0:q0 + nsub * 128], yt.ap[:, 0:nsub * 128],
                    w=[('yT', branch * 2 + kc, q0 // 512)])

        QCH = [(0, 2)] + [(2 + 4 * i, 4) for i in range(4)]

        def attn_core(QT, KT, VA, hmlist, scale, PTB, on_done):
            for qc, (qt0, nsub) in enumerate(QCH):
                q0 = qt0 * 128
                nq = nsub * 128
                ktiles = [0, 1] if qc == 0 else list(range(NT))
                for idx, (cq, ck, pb, K, vh) in enumerate(hmlist):
                    ob = 3 + (idx % 3)
                    Ov = PS[ob][:, 0:260].rearrange('p (j d) -> p j d', d=65)

                    def qk(kt):
                        sb_ = psum(3)
                        mm(PS[sb_][:, 0:nq], KT.ap[pb:pb + K, ck, kt * 128:(kt + 1) * 128], QT.ap[pb:pb + K, cq, q0:q0 + nq], True, True,
                           KT.k2(ck, kt * 128, (kt + 1) * 128) + QT.k2(cq, q0, q0 + nq), [PSK[sb_]])
                        return sb_
                    nxt = qk(ktiles[0])
                    for ki, kt in enumerate(ktiles):
                        sb = nxt
                        if ki + 1 < len(ktiles):
                            nxt = qk(ktiles[ki + 1])
                        pt = PTB[ptr[0] % len(PTB)]
                        ptr[0] += 1
                        act(pt.ap[:, 0:nq], PS[sb][:, 0:nq], AF.Exp, [PSK[sb]], pt.k(), scale=scale)
                        if KATTN < 2:
                            continue
                        for j in range(nsub):
                            mm(Ov[:, j, :], pt.ap[:, j * 128:(j + 1) * 128], VA.ap[:, kt, vh, :], ki == 0 and j == 0, ki == len(ktiles) - 1,
                               pt.k() + VA.k2(kt), [PSK[ob]], skip=True)
                    if KATTN >= 3:
                        on_done(qc, idx, ob, Ov, q0, nsub)

        ptr = [0]

        def mixer_diff(l, lam_init):
            QT = Buf(RA, 0, (3, T), BF16)
            KT = Buf(RA, 13824, (3, T), BF16)
            VA = Buf(RA, 27648, (NT, 4, 65), BF16)
            BV = Buf(RA, 37120, (256,), F32)
            SUB = Buf(RA, 38144, (64,), F32)
            DL = Buf(RA, 38656, (128,), F32)
            TB = [[Buf(RA, 39168 + 4096 * i + 2048 * s, (512,), F32) for s in range(2)] for i in range(2)]
            TMP = [Buf(RA, 47360 + 2048 * i, (512,), F32) for i in range(2)]
            PTB = [Buf(RA, 39168 + 1024 * i, (512,), BF16) for i in range(4)]
            OB = Buf(RA, 43264, (4, 256), F32)
            YB = Buf(RA, 47360, (4, 256), BF16)
            SM = Buf(RA, 49408, (64,), F32)
            T2 = Buf(RA, 49920, (4, 64), F32)
            YT2 = [Buf(RW, 28672 + 1024 * i, (512,), BF16) for i in range(2)]
            pf = PFM.ap
            bcast_row(DL, 'dlam', 128)
            bcast_row(SUB, 'subln', 64)
            bcast_row(BV, 'binv', 256)
            ts(SUB.ap, SUB.ap, 1.0 - lam_init, None, ALU.mult, None, SUB.k(), SUB.k())
            for z in range(2):
                tt(SM.ap[:, 0:32], DL.ap[:, z * 64:z * 64 + 32], DL.ap[:, z * 64 + 32:z * 64 + 64], ALU.mult, DL.k() + SM.k(), SM.k())
                S.op('dve', lambda e, z=z: e.tensor_reduce(out=LAMB.ap[:, z:z + 1], in_=SM.ap[:, 0:32], axis=AX.X, op=ALU.add),
                     SM.k(), LAMB.k())
            act(LAMB.ap[:, 0:2], LAMB.ap[:, 0:2], AF.Exp, LAMB.k(), LAMB.k())
            tt(LAMB.ap[:, 2:3], LAMB.ap[:, 0:1], LAMB.ap[:, 1:2], ALU.subtract, LAMB.k(), LAMB.k())
            ts(LAMB.ap[:, 3:4], LAMB.ap[:, 2:3], lam_init, -1.0, ALU.add, ALU.mult, LAMB.k(), LAMB.k())
            if stop == 'diff0':
                return
            Wq = load_win(4 * 128, 768, 0)
            Wk = load_win(10 * 128, 768, 12288)
            Wv = load_win(NFM * 128, 256, 24576)
            for tq in range(5):
                t0, t1 = TQ[tq]
                n = t1 - t0
                tb = TB[tq % 2]
                ld(tb[0], tb[0].ap[:, 0:n], di['rope'][0, :, t0:t1])
                ld(tb[1], tb[1].ap[:, 0:n], di['rope'][1, :, t0:t1])
                for (W, dst, cbase) in ((Wq, QT, 4), (Wk, KT, 10)):
                    for c in range(3):
                        b1 = psum()
                        proj_fm(W, c, tq, b1)
                        b2 = psum()
                        proj_fm(W, 3 + c, tq, b2)
                        bc = PFM_OFF['binfm'] + cbase + c
                        bp = PFM_OFF['binfm'] + cbase + 3 + c
                        stt(TMP[0].ap[:, 0:n], PS[b1][:, 0:n], pf[:, bc:bc + 1], tb[0].ap[:, 0:n], ALU.add, ALU.mult,
                            [PSK[b1]] + PFM.k() + tb[0].k(), TMP[0].k())
                        stt(TMP[1].ap[:, 0:n], PS[b2][:, 0:n], pf[:, bp:bp + 1], tb[1].ap[:, 0:n], ALU.add, ALU.mult,
                            [PSK[b2]] + PFM.k() + tb[1].k(), TMP[1].k())
                        tt(dst.ap[:, c, t0:t1], TMP[0].ap[:, 0:n], TMP[1].ap[:, 0:n], ALU.add, TMP[0].k() + TMP[1].k(), dst.k2(c, t0, t1))
            if stop == 'diff1':
                return
            v_proj(Wv, 0, 4, VA, BV, 0)
            if stop == 'diff2':
                return
            hml = []
            for h in range(4):
                for m in range(2):
                    hm = h * 2 + m
                    hml.append((hm // 3, hm // 3, (hm % 3) * 32, 32, h))

            def done(qc, idx, ob, Ov, q0, nsub):
                h, m = idx // 2, idx % 2
                if m == 0:
                    done.o1 = (ob, Ov)
                    return
                ob1, Ov1 = done.o1
                S.op('dve', lambda e: e.reciprocal(out=SM.ap[:, 0:nsub], in_=Ov1[:, 0:nsub, 64]), [PSK[ob1]], SM.k())
                S.op('dve', lambda e: e.reciprocal(out=SM.ap[:, 4:4 + nsub], in_=Ov[:, 0:nsub, 64]), [PSK[ob]], SM.k())
                ts(SM.ap[:, 4:4 + nsub], SM.ap[:, 4:4 + nsub], LAMB.ap[:, 3:4], None, ALU.mult, None, SM.k() + LAMB.k(), SM.k())
                for j in range(nsub):
                    ts(T2.ap[:, j, :], Ov[:, j, 0:64], SM.ap[:, 4 + j:5 + j], None, ALU.mult, None, [PSK[ob]] + SM.k(), T2.k())
                    stt(OB.ap[:, j, h * 64:(h + 1) * 64], Ov1[:, j, 0:64], SM.ap[:, j:j + 1], T2.ap[:, j, :], ALU.mult, ALU.add,
                        [PSK[ob1]] + SM.k() + T2.k(), OB.k())
                if h == 3:
                    for j in range(nsub):
                        tt(T2.ap[:, :, :], OB.ap[:, j, :].rearrange('p (h d) -> p h d', d=64),
                           OB.ap[:, j, :].rearrange('p (h d) -> p h d', d=64), ALU.mult, OB.k(), T2.k())
                        S.op('dve', lambda e, j=j: e.tensor_reduce(out=SM.ap[:, 8 + 4 * j:12 + 4 * j], in_=T2.ap[:, :, :], axis=AX.X, op=ALU.add),
                             T2.k(), SM.k())
                    act(SM.ap[:, 8:8 + 4 * nsub], SM.ap[:, 8:8 + 4 * nsub], AF.Sqrt, SM.k() + EPSB.k(), SM.k(), bias=EPSB.ap[:, 0:1], scale=1.0 / 64)
                    S.op('dve', lambda e: e.reciprocal(out=SM.ap[:, 8:8 + 4 * nsub], in_=SM.ap[:, 8:8 + 4 * nsub]), SM.k(), SM.k())
                    for j in range(nsub):
                        for hh in range(4):
                            stt(YB.ap[:, j, hh * 64:(hh + 1) * 64], OB.ap[:, j, hh * 64:(hh + 1) * 64], SM.ap[:, 8 + 4 * j + hh:9 + 4 * j + hh],
                                SUB.ap, ALU.mult, ALU.mult, OB.k() + SM.k() + SUB.k(), YB.k())
                    if KATTN >= 4:
                        y_out(YB, 1, q0, nsub, YT2)
            attn_core(QT, KT, VA, hml, 32 ** -0.5, PTB, done)

        def mixer_gqa(l):
            QT = Buf(RA, 0, (2, T), BF16)
            KT = Buf(RA, 9216, (2, T), BF16)
            VA = Buf(RA, 18432, (NT, 2, 65), BF16)
            BV = Buf(RA, 23552, (128,), F32)
            TB = [[Buf(RA, 24064 + 4096 * i + 2048 * s, (512,), F32) for s in range(2)] for i in range(2)]
            TMP = [Buf(RA, 32256 + 2048 * i, (512,), F32) for i in range(5)]
            PTB = [Buf(RA, 42496 + 1024 * i, (512,), BF16) for i in range(4)]
            YB = Buf(RA, 46592, (4, 256), BF16)
            SM = Buf(RA, 48640, (64,), F32)
            YT2 = [Buf(RW, 20480 + 1024 * i, (512,), BF16) for i in range(2)]
            pf = PFM.ap
            o_ = PROW_OFF['binv'] + 256
            ld(BV, BV.ap, di['prow'][l, o_:o_ + 128].partition_broadcast(128))
            Wq = load_win(16 * 128, 512, 0)
            Wk = load_win(20 * 128, 512, 8192)
            Wv = load_win(NFM * 128 + 256, 128, 16384)
            for tq in range(5):
                t0, t1 = TQ[tq]
                n = t1 - t0
                tb = TB[tq % 2]
                ld(tb[0], tb[0].ap[:, 0:n], di['rope'][2, :, t0:t1])
                ld(tb[1], tb[1].ap[:, 0:n], di['rope'][3, :, t0:t1])
                for (W, dst, cbase, gn, gpn) in ((Wq, QT, 16, 'gq', 'gqp'), (Wk, KT, 20, 'gk', 'gkp')):
                    for c in range(2):
                        b1 = psum()
                        proj_fm(W, c, tq, b1)
                        b2 = psum()
                        proj_fm(W, 2 + c, tq, b2)
                        bc = PFM_OFF['binfm'] + cbase + c
                        bp = PFM_OFF['binfm'] + cbase + 2 + c
                        zb, zpb, sq, t1_, t2_ = TMP
                        act(zb.ap[:, 0:n], PS[b1][:, 0:n], AF.Identity, [PSK[b1]] + PFM.k(), zb.k(), bias=pf[:, bc:bc + 1], scale=1.0)
                        act(sq.ap[:, 0:n], PS[b1][:, 0:n], AF.Square, [PSK[b1]] + PFM.k(), sq.k(), bias=pf[:, bc:bc + 1], scale=1.0)
                        act(zpb.ap[:, 0:n], PS[b2][:, 0:n], AF.Identity, [PSK[b2]] + PFM.k(), zpb.k(), bias=pf[:, bp:bp + 1], scale=1.0)
                        b3 = psum()
                        mm(PS[b3][:, 0:n], BONES.ap, sq.ap[:, 0:n], True, True, BONES.k() + sq.k(), [PSK[b3]])
                        act(sq.ap[:, 0:n], PS[b3][:, 0:n], AF.Ln, [PSK[b3]] + EPSB.k(), sq.k(), bias=EPSB.ap[:, 0:1], scale=1.0 / 64)
                        act(sq.ap[:, 0:n], sq.ap[:, 0:n], AF.Exp, sq.k(), sq.k(), scale=-0.5)
                        g1 = PFM_OFF[gn]
                        g2 = PFM_OFF[gpn]
                        stt(t1_.ap[:, 0:n], zb.ap[:, 0:n], pf[:, g1:g1 + 1], tb[0].ap[:, 0:n], ALU.mult, ALU.mult,
                            zb.k() + PFM.k() + tb[0].k(), t1_.k())
                        stt(t2_.ap[:, 0:n], zpb.ap[:, 0:n], pf[:, g2:g2 + 1], tb[1].ap[:, 0:n], ALU.mult, ALU.mult,
                            zpb.k() + PFM.k() + tb[1].k(), t2_.k())
                        tt(t1_.ap[:, 0:n], t1_.ap[:, 0:n], t2_.ap[:, 0:n], ALU.add, t1_.k() + t2_.k(), t1_.k())
                        tt(dst.ap[:, c, t0:t1], t1_.ap[:, 0:n], sq.ap[:, 0:n], ALU.mult, t1_.k() + sq.k(), dst.k2(c, t0, t1))
            v_proj(Wv, 0, 2, VA, BV, 0)
            hml = []
            for qh in range(4):
                hml.append((qh // 2, qh // 2, (qh % 2) * 64, 64, qh // 2))

            def done(qc, idx, ob, Ov, q0, nsub):
                S.op('dve', lambda e: e.reciprocal(out=SM.ap[:, 0:nsub], in_=Ov[:, 0:nsub, 64]), [PSK[ob]], SM.k())
                for j in range(nsub):
                    ts(YB.ap[:, j, idx * 64:(idx + 1) * 64], Ov[:, j, 0:64], SM.ap[:, j:j + 1], None, ALU.mult, None,
                       [PSK[ob]] + SM.k(), YB.k())
                if idx == 3:
                    y_out(YB, 2, q0, nsub, YT2)
            attn_core(QT, KT, VA, hml, 0.125, PTB, done)

        def mixer_na(l):
            QT = Buf(RA, 0, (2, T), BF16)
            KT = Buf(RA, 9216, (2, T), BF16)
            VA = Buf(RA, 18432, (NT, 4, 65), BF16)
            BV = Buf(RA, 27904, (256,), F32)
            NB = [Buf(RA, 28928 + 5120 * i, (5, 512), BF16) for i in range(2)]
            PTB = [Buf(RA, 39168 + 1024 * i, (512,), BF16) for i in range(4)]
            YB = Buf(RA, 43264, (4, 256), BF16)
            SM = Buf(RA, 45312, (64,), F32)
            YT2 = [Buf(RW, 20480 + 1024 * i, (512,), BF16) for i in range(2)]
            pf = PFM.ap
            o_ = PROW_OFF['binv'] + 384
            ld(BV, BV.ap, di['prow'][l, o_:o_ + 256].partition_broadcast(128))
            Wqk = load_win(24 * 128, 512, 0)
            Wv = load_win(NFM * 128 + 384, 256, 8192)
            for tq in range(5):
                t0, t1 = TQ[tq]
                for c in range(2):
                    b1 = psum()
                    n = proj_fm(Wqk, c, tq, b1)
                    bc = PFM_OFF['binfm'] + 24 + c
                    ts(QT.ap[:, c, t0:t1], PS[b1][:, 0:n], pf[:, bc:bc + 1], 0.125, ALU.add, ALU.mult, [PSK[b1]] + PFM.k(), QT.k2(c, t0, t1))
                    b2 = psum()
                    proj_fm(Wqk, 2 + c, tq, b2)
                    bk = PFM_OFF['binfm'] + 26 + c
                    act(KT.ap[:, c, t0:t1], PS[b2][:, 0:n], AF.Identity, [PSK[b2]] + PFM.k(), KT.k2(c, t0, t1), bias=pf[:, bk:bk + 1], scale=1.0)
            v_proj(Wv, 0, 4, VA, BV, 0)
            cur_v = -1
            nbi = 0
            nas = [0]
            for qt in range(NT):
                q0 = qt * 128
                if qt < 2:
                    ktl = [(0, None), (1, None)]
                else:
                    r = 2 * (qt - 2)
                    v = _na_variant(r)
                    if v != cur_v:
                        nbi += 1
                        ld(NB[nbi % 2], NB[nbi % 2].ap, di['na_bias'][l, v].rearrange('p (t n) -> p t n', n=512))
                        cur_v = v
                    nb = NB[nbi % 2]
                    sr = _na_sr(r)
                    ktl = [(0, None), (1, None)] + [(2 + sr // 2 + i, i) for i in range(5)]
                ob = 4 + (qt % 2)
                Ov = PS[ob][:, 0:260].rearrange('p (j d) -> p j d', d=65)

                def na_qk(kt, bi):
                    sbs_ = []
                    for g in range(2):
                        sb = nas[0] % 4
                        nas[0] += 1
                        sbs_.append(sb)
                        if bi is not None:
                            mm(PS[sb][:, 0:256], IDB.ap, nb.ap[:, bi, g * 256:(g + 1) * 256], True, False, IDB.k() + nb.k2(bi), [PSK[sb]])
                        for s2 in range(2):
                            h = NA_HEAD_ORDER[g * 2 + s2]
                            c, pb = h // 2, (h % 2) * 64
                            mm(PS[sb][:, s2 * 128:(s2 + 1) * 128], KT.ap[pb:pb + 64, c, kt * 128:(kt + 1) * 128],
                               QT.ap[pb:pb + 64, c, q0:q0 + 128], bi is None, (bi is None) or s2 == 1,
                               KT.k2(c, kt * 128, (kt + 1) * 128) + QT.k2(c, q0, q0 + 128), [PSK[sb]])
                    return sbs_
                nxt = na_qk(*ktl[0])
                for ki, (kt, bi) in enumerate(ktl):
                    pt = PTB[ptr[0] % 4]
                    ptr[0] += 1
                    sbs = nxt
                    if ki + 1 < len(ktl):
                        nxt = na_qk(*ktl[ki + 1])
                    for g in range(2):
                        act(pt.ap[:, g * 256:(g + 1) * 256], PS[sbs[g]][:, 0:256], AF.Exp, [PSK[sbs[g]]], pt.k())
                    for h in range(4):
                        slot = NA_HEAD_ORDER.index(h)
                        mm(Ov[:, h, :], pt.ap[:, slot * 128:(slot + 1) * 128], VA.ap[:, kt, h, :], ki == 0 and h == 0, ki == len(ktl) - 1,
                           pt.k() + VA.k2(kt), [PSK[ob]], skip=True)
                S.op('dve', lambda e, Ov=Ov: e.reciprocal(out=SM.ap[:, 0:4], in_=Ov[:, 0:4, 64]), [PSK[ob]], SM.k())
                j = qt % 4 if qt >= 2 else qt
                for h in range(4):
                    ts(YB.ap[:, (qt - 2) % 4 if qt >= 2 else qt, h * 64:(h + 1) * 64], Ov[:, h, 0:64], SM.ap[:, h:h + 1], None, ALU.mult, None,
                       [PSK[ob]] + SM.k(), YB.k())
                if qt == 1:
                    y_out(YB, 3, 0, 2, YT2)
                elif qt >= 2 and (qt - 2) % 4 == 3:
                    y_out(YB, 3, (qt - 3) * 128, 4, YT2)

        def merge(l):
            YTA = Buf(RA, 0, (8, T), BF16)
            MT = Buf(RA, 36864, (T,), BF16)
            SG = [Buf(RA, 41472 + 2048 * i, (512,), F32) for i in range(2)]
            MA = [Buf(RA, 45568 + 2048 * i, (512,), F32) for i in range(2)]
            TP = Buf(RA, 49664, (512,), F32)
            pf = PFM.ap
            ld(YTA, YTA.ap, yT_d.rearrange('c p t -> p c t'), r=[('yT', ch, tq) for ch in range(8) for tq in range(5)])
            for dc in range(8):
                Wg = load_win(GOFF + dc * 512, 512, 8192 * (dc % 2))
                WB = Buf(RW, 16384 + 2048 * (dc % 2), (4, 2, 128), BF16)
                ld(WB, WB.ap, di['w_brx'][l, dc].rearrange('p (j k n) -> p j k n', j=4, k=2))
                for tq in range(5):
                    t0, t1 = TQ[tq]
                    n = t1 - t0
                    ma = MA[tq % 2]
                    for j in range(4):
                        bg = psum(3)
                        proj_fm(Wg, j, tq, bg)
                        by = 3 + psum(3)
                        for k2 in range(2):
                            mm(PS[by][:, 0:n], WB.ap[:, j, k2, :], YTA.ap[:, j * 2 + k2, t0:t1], k2 == 0, k2 == 1,
                               WB.k() + YTA.k2(j * 2 + k2, t0, t1), [PSK[by]])
                        sg = SG[j % 2]
                        bc = PFM_OFF['bgate'] + dc * 4 + j
                        act(sg.ap[:, 0:n], PS[bg][:, 0:n], AF.Sigmoid, [PSK[bg]] + PFM.k(), sg.k(), bias=pf[:, bc:bc + 1], scale=1.0)
                        if j == 0:
                            tt(ma.ap[:, 0:n], sg.ap[:, 0:n], PS[by][:, 0:n], ALU.mult, sg.k() + [PSK[by]], ma.k())
                        else:
                            tt(TP.ap[:, 0:n], sg.ap[:, 0:n], PS[by][:, 0:n], ALU.mult, sg.k() + [PSK[by]], TP.k())
                            if j < 3:
                                tt(ma.ap[:, 0:n], ma.ap[:, 0:n], TP.ap[:, 0:n], ALU.add, ma.k() + TP.k(), ma.k())
                            else:
                                tt(MT.ap[:, t0:t1], ma.ap[:, 0:n], TP.ap[:, 0:n], ALU.add, ma.k() + TP.k(), MT.k(t0, t1))
                stq(MT, mT_d[dc], MT.ap, w=[('mT', dc)])

        def make_gate_tiles(l, g, bname, badaname, base, woff):
            G = [Buf(base[0], base[1] + 4096 * j, (D,), F32) for j in range(2)]
            GB = [Buf(base[0], base[1] + 8192 + 4096 * j, (D,), F32) for j in range(2)]
            LNG = Buf(base[0], base[1] + 16384, (D,), F32)
            LNB = Buf(base[0], base[1] + 20480, (D,), F32)
            W = Buf(RW, woff, (8, 1024), BF16)
            ld(W, W.ap, di['w_ada'][l, :, g * 1024:(g + 1) * 1024].rearrange('(kc p) n -> p kc n', p=128))
            bcast_row(LNG, badaname, 1024)
            bcast_row(LNB, bname, 1024)
            for j in range(2):
                for hf in range(2):
                    b = psum()
                    for kc in range(8):
                        mm(PS[b][:, 0:512], SCB.ap[:, kc, j, :], W.ap[:, kc, hf * 512:(hf + 1) * 512], kc == 0, kc == 7,
                           SCB.k() + W.k2(kc, hf * 512, (hf + 1) * 512), [PSK[b]])
                    tt(G[j].ap[:, hf * 512:(hf + 1) * 512], PS[b][:, 0:512], LNG.ap[:, hf * 512:(hf + 1) * 512], ALU.add,
                       [PSK[b]] + LNG.k(hf * 512, (hf + 1) * 512), G[j].k(hf * 512, (hf + 1) * 512))
                tt(GB[j].ap, G[j].ap, LNB.ap, ALU.mult, G[j].k() + LNB.k(), GB[j].k())
            return G, GB, LNG, LNB

        def post_tile(i, pairs, G, GB, LNG, LNB, TMPB):
            j = 1 if i < 2 else 0
            for (b, hf) in pairs:
                sl = slice(hf * 512, (hf + 1) * 512)
                tb = TMPB[hf]
                tt(tb.ap, PS[b][:, 0:512], G[j].ap[:, sl], ALU.mult, [PSK[b]] + G[j].k(hf * 512, (hf + 1) * 512), tb.k())
                stt(XS.ap[:, i, sl], XS.ap[:, i, sl], ALPHA, tb.ap, ALU.mult, ALU.add, XS.k2(i, hf * 512, (hf + 1) * 512) + tb.k(),
                    XS.k2(i, hf * 512, (hf + 1) * 512))
            tt(XS.ap[:, i, :], XS.ap[:, i, :], GB[j].ap, ALU.add, XS.k2(i) + GB[j].k(), XS.k2(i))
            ln_stats(i)
            ts(XS.ap[:, i, :], XS.ap[:, i, :], MV.ap[:, i, 0:1], RSTD.ap[:, i:i + 1], ALU.subtract, ALU.mult,
               XS.k2(i) + MV.k() + RSTD.k(), XS.k2(i))
            tt(XS.ap[:, i, :], XS.ap[:, i, :], LNG.ap, ALU.mult, XS.k2(i) + LNG.k(), XS.k2(i))
            tt(XS.ap[:, i, :], XS.ap[:, i, :], LNB.ap, ALU.add, XS.k2(i) + LNB.k(), XS.k2(i))

        def wout_phase(l):
            G, GB, LNG, LNB = make_gate_tiles(l, 2, 'bout', 'badag1', (RA, 0), 16384)
            bcast_row(LNG, 'ln1g', 1024)
            bcast_row(LNB, 'ln1b', 1024)
            TMPB = [Buf(RA, 24576 + 2048 * i, (512,), F32) for i in range(2)]
            WO = Buf(RW, 0, (8, D), BF16)
            ld(WO, WO.ap, di['w_out'][l].rearrange('(kc p) n -> p kc n', p=128))
            MB = [Buf(RW, 16384 + 8192 * i, (8, 512), BF16) for i in range(2)]
            for blk in range(5):
                i0 = blk * 4
                nt = min(4, NT - i0)
                mb = MB[blk % 2]
                ld(mb, mb.ap[:, :, 0:nt * 128], mT_d[:, :, i0 * 128:(i0 + nt) * 128].rearrange('c p t -> p c t'),
                   r=[('mT', dc) for dc in range(8)])
                for ii in range(nt):
                    i = i0 + ii
                    pairs = []
                    for hf in range(2):
                        b = psum()
                        for dc in range(8):
                            mm(PS[b][:, 0:512], mb.ap[:, dc, ii * 128:(ii + 1) * 128], WO.ap[:, dc, hf * 512:(hf + 1) * 512], dc == 0, dc == 7,
                               mb.k2(dc, ii * 128, (ii + 1) * 128) + WO.k2(dc, hf * 512, (hf + 1) * 512), [PSK[b]])
                        pairs.append((b, hf))
                    post_tile(i, pairs, G, GB, LNG, LNB, TMPB)
                    ln_to_hT(i, 2)

        FB = [(0, CT)] + [(CT + 410 * i, min(CT + 410 * (i + 1), T)) for i in range(5)]

        def ffn1(l):
            UB = [[Buf(RA, 4096 * (2 * i + s), (1024,), F32) for s in range(2)] for i in range(2)]
            CV = [[Buf(RA, 16384 + 4096 * (2 * i + s), (1024,), F32) for s in range(2)] for i in range(2)]
            AT = [Buf(RA, 32768 + 4608 * i, (T,), BF16) for i in range(2)]
            pf = PFM.ap
            for fc in range(NFC):
                W = Buf(RW, 4096 * (fc % 8), (8, 256), BF16)
                ld(W, W.ap, di['ffn_w1x'][l, :, fc * 256:(fc + 1) * 256].rearrange('(kc p) n -> p kc n', p=128))
                at = AT[fc % 2]
                for bi, (t0, t1) in enumerate(FB):
                    n = t1 - t0
                    lo = t0 - 1 if bi not in (0, 1) else t0
                    hi = t1 + 1 if bi not in (0, 5) else t1
                    padl = 1 if lo == t0 else 0
                    nn = hi - lo
                    ub = UB[bi % 2]
                    cv = CV[bi % 2]
                    for s in range(2):
                        b = psum()
                        for kc in range(8):
                            mm(PS[b][:, 0:nn], W.ap[:, kc, s * 128:(s + 1) * 128], HT.ap[:, kc, lo:hi], kc == 0, kc == 7,
                               W.k2(kc, s * 128, (s + 1) * 128) + HT.k2(kc, lo, hi), [PSK[b]])
                        u = ub[s]
                        col = s * NFC + fc
                        bcol = PFM_OFF['fb1'] + col
                        if padl:
                            mset(u.ap[:, 0:1], 0.0, u.k())
                        if hi == t1:
                            mset(u.ap[:, padl + nn:padl + nn + 1], 0.0, u.k())
                        act(u.ap[:, padl:padl + nn], PS[b][:, 0:nn], AF.Identity, [PSK[b]] + PFM.k(), u.k(), bias=pf[:, bcol:bcol + 1], scale=1.0)
                        w0 = PFM_OFF['fcw'] + 0 * 44 + col
                        w1 = PFM_OFF['fcw'] + 1 * 44 + col
                        w2 = PFM_OFF['fcw'] + 2 * 44 + col
                        cbc = PFM_OFF['fcb'] + col
                        c_ = cv[s]
                        act(c_.ap[:, 0:n], u.ap[:, 0:n], AF.Identity, u.k() + PFM.k(), c_.k(), bias=pf[:, cbc:cbc + 1], scale=pf[:, w0:w0 + 1])
                        stt(c_.ap[:, 0:n], u.ap[:, 1:n + 1], pf[:, w1:w1 + 1], c_.ap[:, 0:n], ALU.mult, ALU.add, u.k() + PFM.k() + c_.k(), c_.k())
                        stt(c_.ap[:, 0:n], u.ap[:, 2:n + 2], pf[:, w2:w2 + 1], c_.ap[:, 0:n], ALU.mult, ALU.add, u.k() + PFM.k() + c_.k(), c_.k())
                    act(cv[1].ap[:, 0:n], cv[1].ap[:, 0:n], AF.Silu, cv[1].k(), cv[1].k())
                    tt(at.ap[:, t0:t1], cv[1].ap[:, 0:n], cv[0].ap[:, 0:n], ALU.mult, cv[0].k() + cv[1].k(), at.k(t0, t1))
                stq(at, aT_d[fc], at.ap, w=[('aT', fc)])

        def ffn2(l, last):
            G, GB, LNG, LNB = make_gate_tiles(l, 5, 'fb2', 'badag2', (RH, 0), 0)
            bcast_row(LNG, 'ln2g', 1024)
            bcast_row(LNB, 'ln2b', 1024)
            TMPB = [Buf(RH, 24576 + 2048 * i, (512,), F32) for i in range(2)]
            W2 = Buf(RA, 0, (NFC, D), BF16)
            ld(W2, W2.ap, di['ffn_w2'][l].rearrange('(fc p) n -> p fc n', p=128))
            AB = [Buf(RW, 11264 * i, (NFC, 256), BF16) for i in range(2)]
            toks = []
            for blk in range(NT // 2):
                ab = AB[blk % 2]
                ld(ab, ab.ap, aT_d[:, :, blk * 256:(blk + 1) * 256].rearrange('c p t -> p c t'),
                   r=[('aT', fc) for fc in range(NFC)])
                for ii in range(2):
                    i = blk * 2 + ii
                    pairs = []
                    for hf in range(2):
                        b = psum()
                        for fc in range(NFC):
                            mm(PS[b][:, 0:512], ab.ap[:, fc, ii * 128:(ii + 1) * 128], W2.ap[:, fc, hf * 512:(hf + 1) * 512], fc == 0, fc == NFC - 1,
                               ab.k2(fc, ii * 128, (ii + 1) * 128) + W2.k2(fc, hf * 512, (hf + 1) * 512), [PSK[b]])
                        pairs.append((b, hf))
                    post_tile(i, pairs, G, GB, LNG, LNB, TMPB)
                    if last and i >= 2:
                        toks.append(stq(XS, out_d[(i - 2) * 128:(i - 1) * 128, :], XS.ap[:, i, :], r=XS.k2(i), sub=1))
            return toks

        out_toks = []
        for l in range(nl):
            l_cur[0] = l
            lam_init = 0.8 - 0.6 * math.exp(-0.3 * l)
            ld(PFM, PFM.ap, di['pfm'][l])
            phase0_ada(l, [0, 1, 3, 4])
            for i in range(NT):
                ln_to_hT(i, 0)
            if stop == 'p1':
                break
            mixer_lru(l)
            if stop == 'lru':
                break
            mixer_diff(l, lam_init)
            if stop in ('diff', 'diff0', 'diff1', 'diff2'):
                break
            mixer_gqa(l)
            if stop == 'gqa':
                break
            mixer_na(l)
            if stop == 'na':
                break
            merge(l)
            if stop == 'merge':
                break
            wout_phase(l)
            if stop == 'wout':
                break
            ffn1(l)
            if stop == 'ffn1':
                break
            out_toks = ffn2(l, l == nl - 1)
        if debug:
            xd = dbg_d.rearrange('(i p) d -> p i d', p=128)
            for i in range(NT):
                out_toks.append(stq(XS, xd[:, i, :], XS.ap[:, i, :], r=XS.k2(i), sub=2))
            for kc in range(8 if stop is not None else 0):
                out_toks.append(stq(HT, dbgh_d[kc], HT.ap[:, kc, :], r=HT.k2(kc), sub=2))
        S.final_wait('sp', out_toks + [t for t in S.last_w.values() if t[2] == 'dma'])
        S.emit(st)
        build.stats = dict(n={e: len(S.q[e]) for e in S.ENGS}, nwaits=S.nwaits, nsems=S.nsems)
    return nc


_CACHE = {}


def kernel(**inputs):
    shared, per_core = prep_inputs(inputs)
    if 'nc' not in _CACHE:
        _CACHE['nc'] = build()
    nc = _CACHE['nc']
    in_maps = []
    for b in range(NCORES):
        m = dict(shared)
        m.update(per_core[b])
        in_maps.append(m)
    res = run_bass_kernel_spmd(nc, in_maps, core_ids=list(range(NCORES)))
    out = np.stack([np.asarray(res.results[b]['out'], np.float32) for b in range(NCORES)], 0)
    return out
```
